# Optimizing a Trainium2 kernel written in Bass

```python
import math
import jax, jax.numpy as jnp
from jax import lax
import numpy as np

D_MODEL = 1024
BATCH = 1
SEQ = 16384
DEPTH = 2

HEAD_DIM = 64
BLOCK_Q = 128
NORM_EPS = 1e-6
MLA_HEADS = 8
MLA_Q_RANK = 256
MLA_KV_RANK = 128
MLA_NOPE = 64
MLA_ROPE = 32
MLA_V = 64
ROPE_BASE = 10000.0
DIFF_HEADS = 4
DIFF_QK = 64
DIFF_V = 2 * DIFF_QK
DIL_CONFIGS = ((128, 1), (512, 4), (2048, 16))
DIL_GROUPS = len(DIL_CONFIGS)
DIL_HEADS_PER_GROUP = 4
SB_HEADS = 4
D_FF = 2816
CONV_WIDTH = 3

EVEN_SPLITS = (MLA_Q_RANK, MLA_KV_RANK, MLA_ROPE,
               DIFF_HEADS * 2 * DIFF_QK, DIFF_HEADS * 2 * DIFF_QK, DIFF_HEADS * DIFF_V)
EVEN_IN = sum(EVEN_SPLITS)
EVEN_OUT = MLA_HEADS * MLA_V + DIFF_HEADS * DIFF_V
DIL_QKV = DIL_GROUPS * DIL_HEADS_PER_GROUP * HEAD_DIM
SB_QKV = SB_HEADS * HEAD_DIM
ODD_SPLITS = (DIL_QKV, DIL_QKV, DIL_QKV, SB_QKV, SB_QKV, SB_QKV)
ODD_IN = sum(ODD_SPLITS)
ODD_OUT = DIL_HEADS_PER_GROUP * HEAD_DIM + SB_QKV
N_EVEN = (DEPTH + 1) // 2
N_ODD = DEPTH // 2

kernel_name = "hybrid_mla_diff_dilated_stickbreak_convffn"


def rms_norm(x, g):
    xf = x.astype(jnp.float32)
    y = xf * lax.rsqrt(jnp.mean(xf * xf, axis=-1, keepdims=True) + NORM_EPS)
    return (y * g.astype(jnp.float32)).astype(x.dtype)


def alibi_slopes(n):
    return jnp.asarray(2.0 ** (-8.0 * np.arange(1, n + 1) / n), dtype=jnp.float32)


def split_cols(p, sizes):
    return jnp.split(p, np.cumsum(sizes)[:-1].tolist(), axis=-1)


def rope(x, pos):
    half = x.shape[-1] // 2
    inv = ROPE_BASE ** (-jnp.arange(half, dtype=jnp.float32) / half)
    ang = pos.astype(jnp.float32)[:, None] * inv
    cos, sin = jnp.cos(ang), jnp.sin(ang)
    xf = x.astype(jnp.float32)
    x1, x2 = xf[..., :half], xf[..., half:]
    return jnp.concatenate([x1 * cos - x2 * sin, x2 * cos + x1 * sin], axis=-1).astype(x.dtype)


def to_blocks(t):
    b, h, s, d = t.shape
    return t.reshape(b, h, s // BLOCK_Q, BLOCK_Q, d).transpose(2, 0, 1, 3, 4)


def from_blocks(t):
    nb, b, h, bq, d = t.shape
    return t.transpose(1, 2, 0, 3, 4).reshape(b, h, nb * bq, d)


def heads_first(t, n_heads):
    b, s, _ = t.shape
    return t.reshape(b, s, n_heads, -1).transpose(0, 2, 1, 3)


def heads_last(t):
    b, h, s, d = t.shape
    return t.transpose(0, 2, 1, 3).reshape(b, s, h * d)


def mla_attention(q_nope, q_rope, k_nope, k_rope, v):
    s_len = v.shape[2]
    scale = (MLA_NOPE + MLA_ROPE) ** -0.5
    kpos = jnp.arange(s_len)

    def block(args):
        i, qn, qr = args
        qpos = i * BLOCK_Q + jnp.arange(BLOCK_Q)
        s = (jnp.einsum('bhqd,bhkd->bhqk', qn, k_nope)
             + jnp.einsum('bhqr,bkr->bhqk', qr, k_rope)).astype(jnp.float32) * scale
        s = jnp.where(kpos[None, :] <= qpos[:, None], s, -jnp.inf)
        p = jax.nn.softmax(s, axis=-1)
        return jnp.einsum('bhqk,bhkd->bhqd', p.astype(v.dtype), v)

    out = lax.map(block, (jnp.arange(s_len // BLOCK_Q), to_blocks(q_nope), to_blocks(q_rope)))
    return from_blocks(out)


def diff_attention(q1, q2, k1, k2, v, lam, slopes):
    s_len = v.shape[2]
    scale = DIFF_QK ** -0.5
    kpos = jnp.arange(s_len)

    def block(args):
        i, a1, a2 = args
        qpos = i * BLOCK_Q + jnp.arange(BLOCK_Q)
        dist = (qpos[:, None] - kpos[None, :]).astype(jnp.float32)
        causal = dist >= 0
        bias = -slopes[:, None, None] * dist

        def probs(qb, kf):
            s = jnp.einsum('bhqd,bhkd->bhqk', qb, kf).astype(jnp.float32) * scale + bias
            return jax.nn.softmax(jnp.where(causal, s, -jnp.inf), axis=-1)

        p = probs(a1, k1) - lam * probs(a2, k2)
        return jnp.einsum('bhqk,bhkd->bhqd', p.astype(v.dtype), v)

    out = lax.map(block, (jnp.arange(s_len // BLOCK_Q), to_blocks(q1), to_blocks(q2)))
    return from_blocks(out)


def dilated_attention(q, k, v, slopes):
    b, g_n, hg, s_len, d = q.shape
    nb = s_len // BLOCK_Q
    scale = d ** -0.5
    qb = q.reshape(b, g_n, hg, nb, BLOCK_Q, d).transpose(3, 0, 1, 2, 4, 5)
    padded = []
    for g, (w, dil) in enumerate(DIL_CONFIGS):
        pad = ((0, 0), (0, 0), (w, 0), (0, 0))
        padded.append((jnp.pad(k[:, g], pad), jnp.pad(v[:, g], pad)))

    def block(args):
        i, qblk = args
        qpos = i * BLOCK_Q + jnp.arange(BLOCK_Q)
        outs, lses = [], []
        for g, (w, dil) in enumerate(DIL_CONFIGS):
            steps = np.arange(w // dil + 1) * dil
            local = np.arange(BLOCK_Q)[:, None] + w - steps[None, :]
            kp, vp = padded[g]
            kb = lax.dynamic_slice_in_dim(kp, i * BLOCK_Q, BLOCK_Q + w, axis=2)
            vb = lax.dynamic_slice_in_dim(vp, i * BLOCK_Q, BLOCK_Q + w, axis=2)
            kg = kb[:, :, local]
            vg = vb[:, :, local]
            steps_f = jnp.asarray(steps, dtype=jnp.float32)
            s = (jnp.einsum('bhqd,bhqnd->bhqn', qblk[:, g], kg).astype(jnp.float32) * scale
                 - slopes[g][:, None, None] * steps_f[None, None, :])
            valid = (qpos[:, None] - jnp.asarray(steps)[None, :]) >= 0
            s = jnp.where(valid, s, -jnp.inf)
            m = jnp.max(s, axis=-1, keepdims=True)
            e = jnp.exp(s - m)
            den = jnp.sum(e, axis=-1, keepdims=True)
            o = jnp.einsum('bhqn,bhqnd->bhqd', (e / den).astype(v.dtype), vg)
            outs.append(o.astype(jnp.float32))
            lses.append((m + jnp.log(den))[..., 0])
        wts = jax.nn.softmax(jnp.stack(lses, axis=0), axis=0)
        o = jnp.einsum('gbhq,gbhqd->bhqd', wts, jnp.stack(outs, axis=0))
        return o.astype(v.dtype)

    out = lax.map(block, (jnp.arange(nb), qb))
    return from_blocks(out)


def stick_breaking_attention(q, k, v):
    s_len = v.shape[2]
    scale = q.shape[-1] ** -0.5
    kpos = jnp.arange(s_len)

    def block(args):
        i, qblk = args
        qpos = i * BLOCK_Q + jnp.arange(BLOCK_Q)
        strict = kpos[None, :] < qpos[:, None]
        z = jnp.einsum('bhqd,bhkd->bhqk', qblk, k).astype(jnp.float32) * scale
        log_beta = jax.nn.log_sigmoid(z)
        log_keep = jnp.where(strict, jax.nn.log_sigmoid(-z), 0.0)
        after = lax.cumsum(log_keep, axis=3, reverse=True) - log_keep
        a = jnp.where(strict, jnp.exp(log_beta + after), 0.0)
        return jnp.einsum('bhqk,bhkd->bhqd', a.astype(v.dtype), v)

    out = lax.map(block, (jnp.arange(s_len // BLOCK_Q), to_blocks(q)))
    return from_blocks(out)


def causal_dwconv(h, w, b):
    c = h.shape[-1]
    y = lax.conv_general_dilated(h, w[:, None, :].astype(h.dtype), (1,), [(CONV_WIDTH - 1, 0)],
                                 dimension_numbers=('NWC', 'WIO', 'NWC'), feature_group_count=c)
    return y + b


def conv_ffn(h, w_up, conv_w, conv_b, w_down):
    u = causal_dwconv(h @ w_up, conv_w, conv_b)
    gate, up = jnp.split(u, 2, axis=-1)
    return (jax.nn.silu(gate) * up) @ w_down


def even_mixer(h, w_in, cq_g, w_uq, ckv_g, w_ukv, lam_q1, lam_k1, lam_q2, lam_k2,
               subln_g, w_out, layer_idx):
    b, s_len, _ = h.shape
    c_q, c_kv, k_rope, dq, dk, dv = split_cols(h @ w_in, EVEN_SPLITS)
    pos = jnp.arange(s_len)
    q = heads_first(rms_norm(c_q, cq_g) @ w_uq, MLA_HEADS)
    q_nope, q_rope = q[..., :MLA_NOPE], rope(q[..., MLA_NOPE:], pos)
    kv = heads_first(rms_norm(c_kv, ckv_g) @ w_ukv, MLA_HEADS)
    k_nope, v_mla = kv[..., :MLA_NOPE], kv[..., MLA_NOPE:]
    o_mla = mla_attention(q_nope, q_rope, k_nope, rope(k_rope, pos), v_mla)
    dq = dq.reshape(b, s_len, DIFF_HEADS, 2, DIFF_QK).transpose(0, 2, 3, 1, 4)
    dk = dk.reshape(b, s_len, DIFF_HEADS, 2, DIFF_QK).transpose(0, 2, 3, 1, 4)
    dv = heads_first(dv, DIFF_HEADS)
    lam_init = 0.8 - 0.6 * math.exp(-0.3 * layer_idx)
    f32 = jnp.float32
    lam = (jnp.exp(jnp.sum(lam_q1.astype(f32) * lam_k1.astype(f32)))
           - jnp.exp(jnp.sum(lam_q2.astype(f32) * lam_k2.astype(f32))) + lam_init)
    o_diff = diff_attention(dq[:, :, 0], dq[:, :, 1], dk[:, :, 0], dk[:, :, 1], dv, lam,
                            alibi_slopes(DIFF_HEADS))
    o_diff = rms_norm(o_diff, subln_g) * (1.0 - lam_init)
    return jnp.concatenate([heads_last(o_mla), heads_last(o_diff)], axis=-1) @ w_out


def odd_mixer(h, w_in, w_out):
    b, s_len, _ = h.shape
    cq, ck, cv, sq, sk, sv = split_cols(h @ w_in, ODD_SPLITS)

    def groups(t):
        return t.reshape(b, s_len, DIL_GROUPS, DIL_HEADS_PER_GROUP, HEAD_DIM).transpose(0, 2, 3, 1, 4)

    slopes = alibi_slopes(DIL_GROUPS * DIL_HEADS_PER_GROUP).reshape(DIL_GROUPS, DIL_HEADS_PER_GROUP)
    o_dil = dilated_attention(groups(cq), groups(ck), groups(cv), slopes)
    o_sb = stick_breaking_attention(heads_first(sq, SB_HEADS), heads_first(sk, SB_HEADS),
                                    heads_first(sv, SB_HEADS))
    return jnp.concatenate([heads_last(o_dil), heads_last(o_sb)], axis=-1) @ w_out


def setup_inputs(seed: int = 0) -> dict:
    key = jax.random.key(seed)
    ks = iter(jax.random.split(key, 32))

    def nrm(shape, scale):
        return jax.random.normal(next(ks), shape, jnp.float32) * scale

    def gain(shape):
        return 1.0 + nrm(shape, 0.05)

    D = D_MODEL
    return {
        "x": nrm((BATCH, SEQ, D), 1.0),
        "ev_pre_g": gain((N_EVEN, D)),
        "ev_w_in": nrm((N_EVEN, D, EVEN_IN), D ** -0.5),
        "ev_cq_g": gain((N_EVEN, MLA_Q_RANK)),
        "ev_w_uq": nrm((N_EVEN, MLA_Q_RANK, MLA_HEADS * (MLA_NOPE + MLA_ROPE)), MLA_Q_RANK ** -0.5),
        "ev_ckv_g": gain((N_EVEN, MLA_KV_RANK)),
        "ev_w_ukv": nrm((N_EVEN, MLA_KV_RANK, MLA_HEADS * (MLA_NOPE + MLA_V)), MLA_KV_RANK ** -0.5),
        "ev_lam_q1": nrm((N_EVEN, DIFF_QK), 0.1),
        "ev_lam_k1": nrm((N_EVEN, DIFF_QK), 0.1),
        "ev_lam_q2": nrm((N_EVEN, DIFF_QK), 0.1),
        "ev_lam_k2": nrm((N_EVEN, DIFF_QK), 0.1),
        "ev_subln_g": gain((N_EVEN, DIFF_V)),
        "ev_w_out": nrm((N_EVEN, EVEN_OUT, D), EVEN_OUT ** -0.5),
        "ev_post_g": gain((N_EVEN, D)),
        "od_pre_g": gain((N_ODD, D)),
        "od_w_in": nrm((N_ODD, D, ODD_IN), D ** -0.5),
        "od_w_out": nrm((N_ODD, ODD_OUT, D), ODD_OUT ** -0.5),
        "od_post_g": gain((N_ODD, D)),
        "ffn_pre_g": gain((DEPTH, D)),
        "ffn_w_up": nrm((DEPTH, D, 2 * D_FF), D ** -0.5),
        "ffn_conv_w": nrm((DEPTH, CONV_WIDTH, 2 * D_FF), CONV_WIDTH ** -0.5),
        "ffn_conv_b": nrm((DEPTH, 2 * D_FF), 0.01),
        "ffn_w_down": nrm((DEPTH, D_FF, D), D_FF ** -0.5),
        "ffn_post_g": gain((DEPTH, D)),
    }


def reference(x, ev_pre_g, ev_w_in, ev_cq_g, ev_w_uq, ev_ckv_g, ev_w_ukv, ev_lam_q1, ev_lam_k1,
              ev_lam_q2, ev_lam_k2, ev_subln_g, ev_w_out, ev_post_g, od_pre_g, od_w_in, od_w_out,
              od_post_g, ffn_pre_g, ffn_w_up, ffn_conv_w, ffn_conv_b, ffn_w_down, ffn_post_g):
    for i in range(DEPTH):
        j = i // 2
        if i % 2 == 0:
            h = rms_norm(x, ev_pre_g[j])
            h = even_mixer(h, ev_w_in[j], ev_cq_g[j], ev_w_uq[j], ev_ckv_g[j], ev_w_ukv[j],
                           ev_lam_q1[j], ev_lam_k1[j], ev_lam_q2[j], ev_lam_k2[j],
                           ev_subln_g[j], ev_w_out[j], i)
            x = x + rms_norm(h, ev_post_g[j])
        else:
            h = rms_norm(x, od_pre_g[j])
            h = odd_mixer(h, od_w_in[j], od_w_out[j])
            x = x + rms_norm(h, od_post_g[j])
        h = conv_ffn(rms_norm(x, ffn_pre_g[i]), ffn_w_up[i], ffn_conv_w[i], ffn_conv_b[i], ffn_w_down[i])
        x = x + rms_norm(h, ffn_post_g[i])
    return x
```

```python
import contextlib
import numpy as np
import ml_dtypes
import concourse.bass as bass
import concourse.mybir as mybir
from concourse.bass_utils import run_bass_kernel_spmd

F32 = mybir.dt.float32
BF16 = mybir.dt.bfloat16
AF = mybir.ActivationFunctionType
ALU = mybir.AluOpType
AX = mybir.AxisListType
NPBF = ml_dtypes.bfloat16

EPOCH = 30000
N_DMA_SEMS = 48


class Op:
    __slots__ = ("eng", "fn", "deps", "is_dma", "idx", "signal", "sem", "val", "dsem_slot")

    def __init__(self, eng, fn, is_dma):
        self.eng = eng
        self.fn = fn
        self.deps = []
        self.is_dma = is_dma
        self.signal = False
        self.sem = None
        self.val = None


class Prog:
    ENGS = ("pe", "act", "dve", "pool", "sp")

    def __init__(self, nc):
        self.nc = nc
        self.ops = {e: [] for e in self.ENGS}
        self.buf = {}
        self.dma_count = 0
        self.dma_last = {}
        self.all_ops = []

    def op(self, eng, fn, reads=(), writes=(), dma=False, extra=()):
        o = Op(eng, fn, dma)
        deps = set()
        for k in reads:
            st = self.buf.get(k)
            if st is not None and st[0] is not None:
                deps.add(st[0])
        for k in writes:
            st = self.buf.get(k)
            if st is not None:
                if st[0] is not None:
                    deps.add(st[0])
                for r in st[1]:
                    deps.add(r)
        for d in extra:
            if d is not None:
                deps.add(d)
        if dma:
            slot = self.dma_count % N_DMA_SEMS
            self.dma_count += 1
            o.dsem_slot = slot
            prev = self.dma_last.get(slot)
            if prev is not None:
                deps.add(prev)
            self.dma_last[slot] = o
        deps.discard(o)
        best = {}
        keep = []
        for d in deps:
            if d.is_dma:
                keep.append(d)
            else:
                b = best.get(d.eng)
                if b is None or d.idx > b.idx:
                    best[d.eng] = d
        o.deps = keep + list(best.values())
        o.idx = len(self.ops[eng])
        for k in reads:
            st = self.buf.setdefault(k, [None, []])
            st[1].append(o)
        for k in writes:
            self.buf[k] = [o, []]
        self.ops[eng].append(o)
        self.all_ops.append(o)
        return o

    def dma(self, out, in_, reads=(), writes=(), eng="sp", **kw):
        return self.op(eng, lambda e: e.dma_start(out=out, in_=in_, **kw), reads, writes, dma=True)

    def emit(self, stack):
        nc = self.nc
        for o in self.all_ops:
            for d in o.deps:
                if d.is_dma:
                    d.signal = True
                elif d.eng != o.eng or o.eng != "pe" or o.is_dma:
                    d.signal = True
        n_sig = {e: 0 for e in self.ENGS}
        for e in self.ENGS:
            for o in self.ops[e]:
                if o.is_dma:
                    o.signal = True
                elif o.signal:
                    n_sig[e] += 1
        sems = {}
        for e in self.ENGS:
            n_ep = (n_sig[e] + EPOCH - 1) // EPOCH
            sems[e] = [stack.enter_context(nc.semaphore(f"s_{e}_{i}")) for i in range(max(n_ep, 1))]
        dsems = [stack.enter_context(nc.semaphore(f"s_dma_{i}")) for i in range(min(N_DMA_SEMS, max(self.dma_count, 1)))]
        dcount = [0] * N_DMA_SEMS
        for e in self.ENGS:
            c = 0
            for o in self.ops[e]:
                if o.is_dma:
                    continue
                if o.signal:
                    o.sem = sems[e][c // EPOCH]
                    o.val = c % EPOCH + 1
                    c += 1
        for o in self.all_ops:
            if o.is_dma:
                dcount[o.dsem_slot] += 16
                o.sem = dsems[o.dsem_slot]
                o.val = dcount[o.dsem_slot]
        block = stack.enter_context(nc.Block())
        engmap = {"pe": "tensor", "act": "scalar", "dve": "vector", "pool": "gpsimd", "sp": "sync"}
        last_dmas = [o for o in self.all_ops if o.is_dma]

        def make_section(e):
            ops = self.ops[e]

            def section(engine):
                waited = {}
                for o in ops:
                    for d in o.deps:
                        if (not d.is_dma) and d.eng == e and e == "pe" and not o.is_dma:
                            continue
                        if d.is_dma:
                            key = ("d", d.dsem_slot)
                            v = d.val
                        else:
                            key = (d.eng, id(d.sem))
                            v = d.val
                        if waited.get(key, 0) >= v:
                            continue
                        engine.wait_ge(d.sem, v)
                        waited[key] = v
                    ins = o.fn(engine)
                    if o.signal:
                        ins.then_inc(o.sem, 16 if o.is_dma else 1)
                if e == "sp":
                    for slot in range(len(dsems)):
                        if dcount[slot] > 0:
                            engine.wait_ge(dsems[slot], dcount[slot])
            return section

        for e in self.ENGS:
            if not self.ops[e] and e != "sp":
                continue
            getattr(block, engmap[e])(make_section(e))


EPS = 1e-6


class Ring:
    def __init__(self, alloc, name, n, shape, dtype):
        self.t = [alloc(f"{name}{i}", shape, dtype) for i in range(n)]
        self.k = [f"{name}{i}" for i in range(n)]
        self.i = 0
        self.n = n

    def next(self):
        j = self.i % self.n
        self.i += 1
        return self.t[j], self.k[j]


class Ctx:
    def __init__(self, nc, st):
        self.nc = nc
        self.st = st
        self.P = Prog(nc)
        self.sb = lambda n, s, d: st.enter_context(nc.sbuf_tensor(n, s, d))
        self.psb = lambda n, s, d: st.enter_context(nc.psum_tensor(n, s, d))
        self.evac_i = 0

    def dram(self, n, s, dt, kind="ExternalInput"):
        return self.nc.dram_tensor(n, s, dt, kind=kind).ap()

    def psum_ring(self, n, name="ps"):
        return Ring(self.psb, name, n, [128, 512], F32)

    def ones_f32(self):
        P = self.P
        t = self.sb("ones_f", [128, 128], F32)
        P.op("pool", lambda e: e.memset(t[:], 1.0), [], ["ones_f"])
        return t

    def evac(self, out, in_, reads, writes):
        self.evac_i += 1
        if self.evac_i % 2:
            return self.P.op("act", lambda e: e.activation(out=out, in_=in_, func=AF.Copy), reads, writes)
        return self.P.op("dve", lambda e: e.tensor_copy(out=out, in_=in_), reads, writes)


def rms_T(C, tag, srcs, src_keys, gains, D, outs, out_keys, W, ones, ps_ring, sq_ring, rs_ring, mul=None):
    P = C.P
    n = len(srcs)
    ps, psk = ps_ring.next()
    for i in range(n):
        sq, sqk = sq_ring.next()
        rows = srcs[i].shape[0]
        P.op("act", lambda e, sq=sq, i=i, rows=rows: e.activation(out=sq[:rows, :W], in_=srcs[i], func=AF.Square), [src_keys[i]], [sqk])
        P.op("pe", lambda e, sq=sq, i=i, rows=rows, ps=ps: e.matmul(ps[:, :W], lhsT=ones[:rows, :], rhs=sq[:rows, :W], start=(i == 0), stop=(i == n - 1)), [sqk, "ones_f"], [psk])
    rs, rsk = rs_ring.next()
    P.op("act", lambda e: e.activation(out=rs[:, :W], in_=ps[:, :W], func=AF.Sqrt, scale=1.0 / D, bias=C.eps_t[:, 0:1]), [psk, "eps_t"], [rsk])
    P.op("dve", lambda e: e.reciprocal(out=rs[:, :W], in_=rs[:, :W]), [rsk], [rsk])
    if mul is not None:
        P.op("dve", lambda e: e.tensor_scalar(out=rs[:, :W], in0=rs[:, :W], scalar1=float(mul), scalar2=None, op0=ALU.mult), [rsk], [rsk])
    for i in range(n):
        rows = srcs[i].shape[0]
        eng = "dve"
        P.op(eng, lambda e, i=i, rows=rows: e.scalar_tensor_tensor(out=outs[i], in0=srcs[i], scalar=gains[i], in1=rs[:rows, :W], op0=ALU.mult, op1=ALU.mult), [src_keys[i], rsk, "gains"], [out_keys[i]])


def build_A(T=2048):
    nc = bass.Bass("TRN2", target_bir_lowering=False)
    W = 512
    NT = T // W
    with contextlib.ExitStack() as st:
        C = Ctx(nc, st)
        P = C.P
        sb = C.sb
        xT = C.dram("xT", [1024, T], F32)
        w_in = C.dram("w_in", [1024, 1952], F32)
        w_kr = C.dram("w_kr", [1024, 2, 96], F32)
        w_uq = C.dram("w_uq", [256, 2, 8, 96], F32)
        w_uk = C.dram("w_uk", [128, 512], F32)
        w_uv = C.dram("w_uv", [128, 512], F32)
        gains_d = C.dram("gains", [128, 11], F32)
        rope_d = C.dram("rope_t", [128, 2, T], F32)
        qT_o = C.dram("qT", [8, 96, T], BF16, "ExternalOutput")
        krT_o = C.dram("krT", [32, T], BF16, "ExternalOutput")
        knT_o = C.dram("knT", [512, T], BF16, "ExternalOutput")
        vm_o = C.dram("vm", [T, 512], BF16, "ExternalOutput")
        dqT_o = C.dram("dqT", [512, T], BF16, "ExternalOutput")
        dkT_o = C.dram("dkT", [512, T], BF16, "ExternalOutput")
        dv_o = C.dram("dv", [T, 512], BF16, "ExternalOutput")

        w_in_b = sb("w_in_b", [128, 8, 1952], BF16)
        w_kr_b = sb("w_kr_b", [128, 8, 2, 96], BF16)
        w_uq_b = sb("w_uq_b", [128, 2, 2, 8, 96], BF16)
        w_uk_b = sb("w_uk_b", [128, 512], BF16)
        w_uv_b = sb("w_uv_b", [128, 512], BF16)
        gains = sb("gains_t", [128, 11], F32)
        rope_t = sb("rope_tt", [128, 2, T], F32)
        C.eps_t = sb("eps_t", [128, 1], F32)
        P.op("pool", lambda e: e.memset(C.eps_t[:], EPS), [], ["eps_t"])
        ones = C.ones_f32()
        w_in_v = w_in.rearrange("(kc p) n -> p kc n", p=128)
        for kc in range(8):
            P.dma(w_in_b[:, kc, :], w_in_v[:, kc, :], writes=[f"w_in{kc}"], eng="pool")
        P.dma(w_kr_b[:], w_kr.rearrange("(kc p) v n -> p kc v n", p=128), writes=["w_kr"], eng="pool")
        P.dma(w_uq_b[:], w_uq.rearrange("(kc p) v h n -> p kc v h n", p=128), writes=["w_uq"], eng="pool")
        P.dma(w_uk_b[:], w_uk[:, :], writes=["w_uk"], eng="pool")
        P.dma(w_uv_b[:], w_uv[:, :], writes=["w_uv"], eng="pool")
        P.dma(gains[:], gains_d[:, :], writes=["gains"])
        P.dma(rope_t[:], rope_d[:, :, :], writes=["rope"])
        w_in_keys = [f"w_in{kc}" for kc in range(8)]

        xr = Ring(sb, "xt", 2, [128, 8, W], F32)
        hnr = Ring(sb, "hn", 2, [128, 8, W], BF16)
        psr = C.psum_ring(6)
        pss = Ring(C.psb, "pss", 2, [128, 512], F32)
        sqr = Ring(sb, "sq", 3, [128, W], F32)
        rsr = Ring(sb, "rs", 2, [128, W], F32)
        stg = Ring(sb, "stg", 4, [128, W], BF16)
        cqr = Ring(sb, "cq", 2, [128, 2, W], F32)
        cqnr = Ring(sb, "cqn", 2, [128, 2, W], BF16)
        ckvr = Ring(sb, "ckv", 2, [128, W], F32)
        ckvnr = Ring(sb, "ckvn", 2, [128, W], BF16)
        tmpr = Ring(sb, "tmp", 4, [128, W], F32)
        qstg = Ring(sb, "qstg", 3, [128, W], BF16)
        xT_v = xT.rearrange("(kc p) t -> p kc t", p=128)

        def rope_evac(psA, psAk, psB, psBk, out_ap, out_key, c0):
            t1, t1k = tmpr.next()
            t2, t2k = tmpr.next()
            P.op("dve", lambda e: e.tensor_tensor(out=t1[64:96, :], in0=psA[64:96, :], in1=rope_t[64:96, 0, c0:c0 + W], op=ALU.mult), [psAk, "rope"], [t1k])
            P.op("dve", lambda e: e.tensor_tensor(out=t2[64:96, :], in0=psB[64:96, :], in1=rope_t[64:96, 1, c0:c0 + W], op=ALU.mult), [psBk, "rope"], [t2k])
            P.op("pool", lambda e: e.tensor_tensor(out=out_ap, in0=t1[64:96, :], in1=t2[64:96, :], op=ALU.add), [t1k, t2k], [out_key])

        def tile_body(tt):
            c0 = tt * W
            xt, xk = xr.next()
            P.dma(xt[:], xT_v[:, :, c0:c0 + W], writes=[xk])
            hn, hk = hnr.next()
            rms_T(C, "pre", [xt[:, kc, :] for kc in range(8)], [xk] * 8, [gains[:, kc:kc + 1] for kc in range(8)], 1024,
                  [hn[:, kc, :] for kc in range(8)], [hk] * 8, W, ones, pss, sqr, rsr)

            def proj(lhs_fn, rows, keys):
                ps, pk = psr.next()
                for kc in range(8):
                    P.op("pe", lambda e, kc=kc, ps=ps: e.matmul(ps[:rows, :], lhsT=lhs_fn(kc), rhs=hn[:, kc, :], start=(kc == 0), stop=(kc == 7)), [hk] + keys, [pk])
                return ps, pk

            cq, cqk = cqr.next()
            for i in range(2):
                ps, pk = proj(lambda kc, i=i: w_in_b[:, kc, 128 * i:128 * i + 128], 128, w_in_keys)
                C.evac(cq[:, i, :], ps[:, :], [pk], [cqk])
            ckv, ckvk = ckvr.next()
            ps, pk = proj(lambda kc: w_in_b[:, kc, 256:384], 128, w_in_keys)
            C.evac(ckv[:, :], ps[:, :], [pk], [ckvk])
            psA, pkA = proj(lambda kc: w_kr_b[:, kc, 0, :], 96, ["w_kr"])
            psB, pkB = proj(lambda kc: w_kr_b[:, kc, 1, :], 96, ["w_kr"])
            so, sk = stg.next()
            rope_evac(psA, pkA, psB, pkB, so[64:96, :], sk, c0)
            P.dma(krT_o[:, c0:c0 + W], so[64:96, :], reads=[sk])
            for base, dst in ((416, dqT_o), (928, dkT_o)):
                for i in range(4):
                    ps, pk = proj(lambda kc, i=i, base=base: w_in_b[:, kc, base + 128 * i:base + 128 * i + 128], 128, w_in_keys)
                    so, sk = stg.next()
                    C.evac(so[:, :], ps[:, :], [pk], [sk])
                    P.dma(dst[128 * i:128 * i + 128, c0:c0 + W], so[:, :], reads=[sk])
            for tb in range(W // 128):
                ps, pk = psr.next()
                for kc in range(8):
                    P.op("pe", lambda e, kc=kc, ps=ps, tb=tb: e.matmul(ps[:, :], lhsT=hn[:, kc, tb * 128:(tb + 1) * 128], rhs=w_in_b[:, kc, 1440:1952], start=(kc == 0), stop=(kc == 7)), [hk] + w_in_keys, [pk])
                so, sk = stg.next()
                C.evac(so[:, :], ps[:, :], [pk], [sk])
                P.dma(dv_o[c0 + tb * 128:c0 + tb * 128 + 128, :], so[:, :], reads=[sk])
            cqn, cqnk = cqnr.next()
            rms_T(C, "cq", [cq[:, i, :] for i in range(2)], [cqk] * 2, [gains[:, 8 + i:9 + i] for i in range(2)], 256,
                  [cqn[:, i, :] for i in range(2)], [cqnk] * 2, W, ones, pss, sqr, rsr)
            for h in range(8):
                pq = []
                for v in range(2):
                    ps, pk = psr.next()
                    for k2 in range(2):
                        P.op("pe", lambda e, k2=k2, ps=ps, v=v, h=h: e.matmul(ps[:96, :], lhsT=w_uq_b[:, k2, v, h, :], rhs=cqn[:, k2, :], start=(k2 == 0), stop=(k2 == 1)), [cqnk, "w_uq"], [pk])
                    pq.append((ps, pk))
                so, sk = qstg.next()
                C.evac(so[0:64, :], pq[0][0][0:64, :], [pq[0][1]], [sk + "a"])
                rope_evac(pq[0][0], pq[0][1], pq[1][0], pq[1][1], so[64:96, :], sk + "b", c0)
                P.dma(qT_o[h, :, c0:c0 + W], so[0:96, :], reads=[sk + "a", sk + "b"], writes=[sk])
            ckvn, ckvnk = ckvnr.next()
            rms_T(C, "ckv", [ckv[:, :]], [ckvk], [gains[:, 10:11]], 128, [ckvn[:, :]], [ckvnk], W, ones, pss, sqr, rsr)
            for i in range(4):
                ps, pk = psr.next()
                P.op("pe", lambda e, ps=ps, i=i: e.matmul(ps[:, :], lhsT=w_uk_b[:, 128 * i:128 * i + 128], rhs=ckvn[:, :], start=True, stop=True), [ckvnk, "w_uk"], [pk])
                so, sk = stg.next()
                C.evac(so[:, :], ps[:, :], [pk], [sk])
                P.dma(knT_o[128 * i:128 * i + 128, c0:c0 + W], so[:, :], reads=[sk])
            for tb in range(W // 128):
                ps, pk = psr.next()
                P.op("pe", lambda e, ps=ps, tb=tb: e.matmul(ps[:, :], lhsT=ckvn[:, tb * 128:(tb + 1) * 128], rhs=w_uv_b[:, :], start=True, stop=True), [ckvnk, "w_uv"], [pk])
                so, sk = stg.next()
                C.evac(so[:, :], ps[:, :], [pk], [sk])
                P.dma(vm_o[c0 + tb * 128:c0 + tb * 128 + 128, :], so[:, :], reads=[sk])
        for tt in range(NT):
            tile_body(tt)
        P.emit(st)
    return nc


S_LEN = 16384


def attn_stream(C, tag, QT, KT, V, rows, dv, scale, out_d, psS, psO, ptr, ostg, rdr, ident, maskb, n_qt=32, q_off=0):
    P = C.P
    dv1 = dv + 1
    per_bank = 512 // dv1 if dv1 * 4 <= 512 else (2 if dv1 * 2 <= 512 else 1)
    per_bank = min(per_bank, 4)
    nbank = (4 + per_bank - 1) // per_bank
    out_v = out_d.rearrange("(t qs p) d -> t p qs d", qs=4, p=128)

    def q_tile(qt):
        banks = [psO.next() for _ in range(nbank)]

        def oslot(qs):
            b, bk = banks[qs // per_bank]
            o = (qs % per_bank) * dv1
            return b[:, o:o + dv1], bk

        n_kb = 4 * qt + 4
        q0 = 512 * qt

        def k_block(kb):
            j = kb - 4 * qt
            diag = j >= 0
            c0 = 128 * j if diag else 0
            s, sk = psS.next()
            ksl = KT[:rows, kb * 128:(kb + 1) * 128]
            if diag:
                P.op("pe", lambda e: e.matmul(s[:, c0:c0 + 128], lhsT=ksl, rhs=QT[:rows, q0 + c0:q0 + c0 + 128], start=True, stop=False), ["QT", "KT"], [sk])
                P.op("pe", lambda e: e.matmul(s[:, c0:c0 + 128], lhsT=ident[:, :], rhs=maskb[:, :], start=False, stop=True), ["ident", "maskb"], [sk])
                if c0 + 128 < 512:
                    P.op("pe", lambda e: e.matmul(s[:, c0 + 128:512], lhsT=ksl, rhs=QT[:rows, q0 + c0 + 128:q0 + 512], start=True, stop=True), ["QT", "KT"], [sk])
            else:
                P.op("pe", lambda e: e.matmul(s[:, :], lhsT=ksl, rhs=QT[:rows, q0:q0 + 512], start=True, stop=True), ["QT", "KT"], [sk])
            pt, ptk = ptr.next()
            P.op("act", lambda e: e.activation(out=pt[:, c0:512], in_=s[:, c0:512], func=AF.Exp, scale=scale), [sk], [ptk])
            for qs in range(c0 // 128, 4):
                o, ok = oslot(qs)
                P.op("pe", lambda e, o=o, qs=qs: e.matmul(o, lhsT=pt[:, qs * 128:(qs + 1) * 128], rhs=V[:, kb, :dv1], start=(kb == 0 and qs % per_bank == 0), stop=(kb == 4 * qt + qs)), [ptk, "V"], [ok])

        for kb in range(n_kb):
            k_block(kb)
        og, ogk = ostg.next()
        rd, rdk = rdr.next()
        for qs in range(4):
            o, ok = oslot(qs)
            P.op("dve", lambda e, o=o, qs=qs: e.reciprocal(out=rd[:, qs:qs + 1], in_=o[:, dv:dv1]), [ok], [rdk + str(qs)])
            P.op("dve", lambda e, o=o, qs=qs: e.tensor_scalar(out=og[:, qs, :dv], in0=o[:, :dv], scalar1=rd[:, qs:qs + 1], scalar2=None, op0=ALU.mult), [ok, rdk + str(qs)], [ogk + str(qs)])
        P.dma(out_v[qt + q_off], og[:, :, :dv], reads=[ogk + str(qs) for qs in range(4)] + [rdk + str(qs) for qs in range(4)], writes=[ogk, rdk])

    for qt in range(n_qt):
        q_tile(qt)


def load_big(C, dst, src, rows, key, ncols, eng_cycle=("sp", "pool"), nsplit=4):
    step = ncols // nsplit
    for i in range(nsplit):
        C.P.dma(dst[:rows, i * step:(i + 1) * step], src[:, i * step:(i + 1) * step], writes=[key], eng=eng_cycle[i % len(eng_cycle)])


def build_B(n_qt=32):
    nc = bass.Bass("TRN2", target_bir_lowering=False)
    S = n_qt * 512
    NKB = S // 128
    with contextlib.ExitStack() as st:
        C = Ctx(nc, st)
        P = C.P
        sb = C.sb
        QT_d = C.dram("QT", [96, S], BF16)
        KT_d = C.dram("KT", [96, S], BF16)
        V_d = C.dram("V1", [128, NKB, 65], BF16)
        dQT_d = C.dram("dQT", [68, S], BF16)
        dKT_d = C.dram("dKT", [68, S], BF16)
        dV_d = C.dram("dV1", [128, NKB, 129], BF16)
        mask_d = C.dram("maskb", [128, 128], BF16)
        id_d = C.dram("ident", [128, 128], BF16)
        om_o = C.dram("o_mla", [S, 64], F32, "ExternalOutput")
        od_o = C.dram("o_d", [S, 128], F32, "ExternalOutput")
        QT = sb("QT_s", [128, S], BF16)
        KT = sb("KT_s", [128, S], BF16)
        Vf = sb("V_s", [128, NKB * 129], BF16)
        V = Vf[:, :].rearrange("p (k d) -> p k d", d=129)
        Vm = Vf[:, :NKB * 65].rearrange("p (k d) -> p k d", d=65)
        ident = sb("ident_s", [128, 128], BF16)
        maskb = sb("maskb_s", [128, 128], BF16)
        P.dma(ident[:], id_d[:, :], writes=["ident"])
        P.dma(maskb[:], mask_d[:, :], writes=["maskb"])
        psS = Ring(C.psb, "psS", 3, [128, 512], F32)
        psO = Ring(C.psb, "psO", 4, [128, 512], F32)
        ptr = Ring(sb, "pt", 3, [128, 512], BF16)
        ostg = Ring(sb, "og", 2, [128, 4, 128], F32)
        rdr = Ring(sb, "rd", 2, [128, 4], F32)
        load_big(C, QT, QT_d, 96, "QT", S)
        load_big(C, KT, KT_d, 96, "KT", S)
        P.dma(Vm, V_d[:, :, :], writes=["V"], eng="pool")
        attn_stream(C, "mla", QT, KT, Vm, 96, 64, 96 ** -0.5, om_o, psS, psO, ptr, ostg, rdr, ident, maskb, n_qt)
        load_big(C, QT, dQT_d, 68, "QT", S)
        load_big(C, KT, dKT_d, 68, "KT", S)
        P.dma(V[:, :, :], dV_d[:, :, :], writes=["V"], eng="pool")
        attn_stream(C, "diff", QT, KT, V, 68, 128, 0.125, od_o, psS, psO, ptr, ostg, rdr, ident, maskb, n_qt)
        P.emit(st)
    return nc


def sb_stream(C, QT, KT, V, M01, Mneg, negtri, negones, ident, one_t, out_d, psA, psB, psO, n_slots=16):
    P = C.P
    sb = C.sb
    e_r = Ring(sb, "sbe", 2, [128, 512], F32)
    sp_r = Ring(sb, "sbsp", 3, [128, 512], BF16)
    a_r = Ring(sb, "sba", 3, [128, 512], BF16)
    ra_r = Ring(sb, "sbra", 2, [128, 512], BF16)
    og_r = Ring(sb, "sbog", 2, [128, 4, 64], F32)
    out_v = out_d.rearrange("(t qs p) d -> t p qs d", qs=4, p=128)

    def slot(i):
        ob, obk = psO.next()
        qsl = QT[:64, i * 512:(i + 1) * 512]
        state = {"ra": None}
        kbs = list(range(8 * i + 7, -1, -1))

        def block(kb, first, last):
            j = kb - 8 * i
            diag = j >= 0
            ksl = KT[:64, kb * 128:(kb + 1) * 128]
            pa, pak = psA.next()
            P.op("pe", lambda e: e.matmul(pa[:, :], lhsT=ksl, rhs=qsl, start=True, stop=True), ["QT", "KT"], [pak])
            et, etk = e_r.next()
            P.op("act", lambda e: e.activation(out=et[:, :], in_=pa[:, :], func=AF.Exp, scale=0.125), [pak], [etk])
            sp, spk = sp_r.next()
            P.op("act", lambda e: e.activation(out=sp[:, :], in_=et[:, :], func=AF.Ln, scale=1.0, bias=one_t[:, 0:1]), [etk, "one_t"], [spk])
            if diag:
                P.op("dve", lambda e: e.tensor_tensor(out=sp[:, :], in0=sp[:, :], in1=M01[:, j, :], op=ALU.mult), [spk, "M01"], [spk])
            pb, pbk = psB.next()
            ra = state["ra"]
            nmm = 2 + (0 if first else 1) + (1 if diag else 0)
            cnt = [0]

            def mm(lhsT, rhs, reads):
                k = cnt[0]
                cnt[0] += 1
                P.op("pe", lambda e: e.matmul(pb[:, :], lhsT=lhsT, rhs=rhs, start=(k == 0), stop=(k == nmm - 1)), reads, [pbk])

            mm(ksl, qsl, ["QT", "KT"])
            mm(negtri[:, :], sp[:, :], [spk, "negtri"])
            if not first:
                mm(negones[:, :], ra[0][:, :], [ra[1], "negones"])
            if diag:
                mm(ident[:, :], Mneg[:, j, :], ["ident", "Mneg"])
            at, atk = a_r.next()
            P.op("act", lambda e: e.activation(out=at[:, :], in_=pb[:, :], func=AF.Exp, scale=0.125), [pbk], [atk])
            for qs in range(4):
                P.op("pe", lambda e, qs=qs: e.matmul(ob[:, qs * 64:(qs + 1) * 64], lhsT=at[:, qs * 128:(qs + 1) * 128], rhs=V[:, kb, :64], start=(first and qs == 0), stop=last), [atk, "V"], [obk])
            if not last:
                rn, rnk = ra_r.next()
                if first:
                    P.op("pool", lambda e: e.tensor_copy(out=rn[:, :], in_=sp[:, :]), [spk], [rnk])
                else:
                    P.op("pool", lambda e: e.tensor_tensor(out=rn[:, :], in0=ra[0][:, :], in1=sp[:, :], op=ALU.add), [ra[1], spk], [rnk])
                state["ra"] = (rn, rnk)

        for n, kb in enumerate(kbs):
            block(kb, n == 0, n == len(kbs) - 1)
        og, ogk = og_r.next()
        C.evac(og[:, :, :], ob[:, :256].rearrange("p (a d) -> p a d", d=64), [obk], [ogk])
        P.dma(out_v[i], og[:, :, :], reads=[ogk])

    for i in range(n_slots):
        slot(i)


def dil_part(C, dQ_d, dK_d, dV_d, Bt, ident_f, out_d, psS, psO, n_heads=12, n_blk=16):
    P = C.P
    sb = C.sb
    q_r = Ring(sb, "dlq", 2, [128, n_blk * 128], BF16)
    k_r = Ring(sb, "dlk", 2, [128, (n_blk + 1) * 128], BF16)
    v_r = Ring(sb, "dlv", 2, [128, n_blk + 1, 128], BF16)
    p_r = Ring(sb, "dlp", 3, [128, 256], BF16)
    og_r = Ring(sb, "dlo", 2, [128, n_blk, 128], F32)

    def head(n):
        q, qk = q_r.next()
        k, kk = k_r.next()
        v, vk = v_r.next()
        P.dma(q[:65, :], dQ_d[n], writes=[qk])
        P.dma(k[:65, :], dK_d[n], writes=[kk], eng="pool")
        P.dma(v[:, :, :], dV_d[n], writes=[vk])
        og, ogk = og_r.next()

        def unit(b):
            s, sk = psS.next()
            qs_ = q[:, b * 128:(b + 1) * 128]
            P.op("pe", lambda e: e.matmul(s[:, 0:128], lhsT=k[:65, b * 128:(b + 1) * 128], rhs=qs_[:65, :], start=True, stop=False), [qk, kk], [sk])
            P.op("pe", lambda e: e.matmul(s[:, 0:128], lhsT=ident_f[:, :], rhs=Bt[:, n, 0, :], start=False, stop=True), ["ident_f", "Bt"], [sk])
            P.op("pe", lambda e: e.matmul(s[:, 128:256], lhsT=k[:64, (b + 1) * 128:(b + 2) * 128], rhs=qs_[:64, :], start=True, stop=False), [qk, kk], [sk])
            P.op("pe", lambda e: e.matmul(s[:, 128:256], lhsT=ident_f[:, :], rhs=Bt[:, n, 1, :], start=False, stop=True), ["ident_f", "Bt"], [sk])
            p, pk = p_r.next()
            P.op("act", lambda e: e.activation(out=p[:, :], in_=s[:, 0:256], func=AF.Exp, scale=0.125), [sk], [pk])
            o, ok = psO.next()
            P.op("pe", lambda e: e.matmul(o[:, 0:128], lhsT=p[:, 0:128], rhs=v[:, b, :], start=True, stop=False), [pk, vk], [ok])
            P.op("pe", lambda e: e.matmul(o[:, 0:128], lhsT=p[:, 128:256], rhs=v[:, b + 1, :], start=False, stop=True), [pk, vk], [ok])
            P.op("dve", lambda e: e.tensor_copy(out=og[:, b, :], in_=o[:, 0:128]), [ok], [ogk + "u"])

        for b in range(n_blk):
            unit(b)
        P.dma(out_d[n].rearrange("b p d -> p b d"), og[:, :, :], reads=[ogk + "u"], writes=[ogk])

    for n in range(n_heads):
        head(n)


def build_D(n_slots=16, n_heads=12, do_sb=True, do_dil=True):
    nc = bass.Bass("TRN2", target_bir_lowering=False)
    S = n_slots * 1024
    NKB = S // 128
    with contextlib.ExitStack() as st:
        C = Ctx(nc, st)
        P = C.P
        sb = C.sb
        QT_d = C.dram("sQT", [64, n_slots * 512], BF16)
        KT_d = C.dram("sKT", [64, S], BF16)
        V_d = C.dram("sV", [128, NKB, 64], BF16)
        M01_d = C.dram("M01", [128, 8, 512], BF16)
        Mneg_d = C.dram("Mneg", [128, 8, 512], BF16)
        tri_d = C.dram("negtri", [128, 128], BF16)
        id_d = C.dram("ident", [128, 128], BF16)
        idf_d = C.dram("ident_f", [128, 128], F32)
        dQ_d = C.dram("dlQ", [n_heads, 65, 16 * 128], BF16)
        dK_d = C.dram("dlK", [n_heads, 65, 17 * 128], BF16)
        dV_d = C.dram("dlV", [n_heads, 128, 17, 128], BF16)
        Bt_d = C.dram("dlB", [128, n_heads, 2, 128], F32)
        osb_o = C.dram("o_sb", [n_slots * 512, 64], F32, "ExternalOutput")
        odl_o = C.dram("o_dl", [n_heads, 16, 128, 128], F32, "ExternalOutput")
        QT = sb("QT_s", [128, n_slots * 512], BF16)
        KT = sb("KT_s", [128, S], BF16)
        V = sb("V_s", [128, NKB, 64], BF16)
        M01 = sb("M01_s", [128, 8, 512], BF16)
        Mneg = sb("Mneg_s", [128, 8, 512], BF16)
        negtri = sb("negtri_s", [128, 128], BF16)
        negones = sb("negones_s", [128, 128], BF16)
        ident = sb("ident_s", [128, 128], BF16)
        ident_f = sb("identf_s", [128, 128], F32)
        Bt = sb("Bt_s", [128, n_heads, 2, 128], F32)
        one_t = sb("one_t", [128, 1], F32)
        P.op("pool", lambda e: e.memset(one_t[:], 1.0), [], ["one_t"])
        P.op("pool", lambda e: e.memset(negones[:], -8.0), [], ["negones"])
        for t, d, k in ((M01, M01_d, "M01"), (Mneg, Mneg_d, "Mneg"), (Bt, Bt_d, "Bt")):
            P.dma(t[:], d[:, :, :] if len(d.shape) == 3 else d[:, :, :, :], writes=[k])
        for t, d, k in ((negtri, tri_d, "negtri"), (ident, id_d, "ident"), (ident_f, idf_d, "ident_f")):
            P.dma(t[:], d[:, :], writes=[k])
        psA = Ring(C.psb, "psA", 2, [128, 512], F32)
        psB = Ring(C.psb, "psB", 2, [128, 512], F32)
        psO = Ring(C.psb, "psO", 2, [128, 512], F32)
        if do_sb:
            load_big(C, QT, QT_d, 64, "QT", n_slots * 512)
            load_big(C, KT, KT_d, 64, "KT", S)
            P.dma(V[:, :, :], V_d[:, :, :], writes=["V"], eng="pool")
            sb_stream(C, QT, KT, V, M01, Mneg, negtri, negones, ident, one_t, osb_o, psA, psB, psO, n_slots)
        if do_dil:
            dil_part(C, dQ_d, dK_d, dV_d, Bt, ident_f, odl_o, psA, psO, n_heads)
        P.emit(st)
    return nc


LAM_INIT0 = 0.8 - 0.6 * 1.0


def build_post(layer, T=2048):
    nc = bass.Bass("TRN2", target_bir_lowering=False)
    W = 512
    NT = T // W
    nA = 12 if layer == 0 else 14
    nK = 8 if layer == 0 else 4
    with contextlib.ExitStack() as st:
        C = Ctx(nc, st)
        P = C.P
        sb = C.sb
        xT = C.dram("xT", [1024, T], F32)
        aT = C.dram("aT", [nA * 128, T], F32)
        w_out = C.dram("w_out", [nK * 128, 1024], F32)
        gains_d = C.dram("gains", [128, 9], F32)
        lam_d = C.dram("lamv", [128, 4, 64], F32)
        xm_o = C.dram("xmT", [1024, T], F32, "ExternalOutput")
        w_b = sb("w_b", [128, nK, 1024], BF16)
        gains = sb("gains_t", [128, 9], F32)
        C.eps_t = sb("eps_t", [128, 1], F32)
        P.op("pool", lambda e: e.memset(C.eps_t[:], EPS), [], ["eps_t"])
        ones = C.ones_f32()
        P.dma(w_b[:], w_out.rearrange("(kc p) n -> p kc n", p=128), writes=["w_b"], eng="pool")
        P.dma(gains[:], gains_d[:, :], writes=["gains"])
        neg_lam = sb("neg_lam", [128, 1], F32)
        if layer == 0:
            lamv = sb("lamv_t", [128, 4, 64], F32)
            pr = sb("lam_pr", [128, 2, 64], F32)
            ls = sb("lam_s", [128, 4], F32)
            P.dma(lamv[:], lam_d[:, :, :], writes=["lamv"])
            for i in range(2):
                P.op("dve", lambda e, i=i: e.tensor_tensor(out=pr[:, i, :], in0=lamv[:, 2 * i, :], in1=lamv[:, 2 * i + 1, :], op=ALU.mult), ["lamv"], ["lam_pr"])
                P.op("dve", lambda e, i=i: e.tensor_reduce(out=ls[:, i:i + 1], in_=pr[:, i, :], axis=AX.X, op=ALU.add), ["lam_pr"], ["lam_s"])
                P.op("act", lambda e, i=i: e.activation(out=ls[:, 2 + i:3 + i], in_=ls[:, i:i + 1], func=AF.Exp), ["lam_s"], ["lam_e"])
            P.op("dve", lambda e: e.tensor_tensor(out=neg_lam[:, :], in0=ls[:, 3:4], in1=ls[:, 2:3], op=ALU.subtract), ["lam_e"], ["neg_lam"])
            P.op("dve", lambda e: e.tensor_scalar(out=neg_lam[:, :], in0=neg_lam[:, :], scalar1=-LAM_INIT0, scalar2=None, op0=ALU.add), ["neg_lam"], ["neg_lam"])
        xr = Ring(sb, "xt", 2, [128, 8, W], F32)
        ar = Ring(sb, "at", 2, [128, nA, W], F32)
        catr = Ring(sb, "cat", 2, [128, nK, W], BF16)
        hr = Ring(sb, "ht", 2, [128, 8, W], F32)
        psr = C.psum_ring(6)
        pss = Ring(C.psb, "pss", 2, [128, 512], F32)
        sqr = Ring(sb, "sq", 3, [128, W], F32)
        rsr = Ring(sb, "rs", 2, [128, W], F32)
        tmpr = Ring(sb, "tmp", 4, [128, W], F32)
        xT_v = xT.rearrange("(kc p) t -> p kc t", p=128)
        aT_v = aT.rearrange("(kc p) t -> p kc t", p=128)
        xm_v = xm_o.rearrange("(kc p) t -> p kc t", p=128)

        def tile_body(tt):
            c0 = tt * W
            xt, xk = xr.next()
            P.dma(xt[:], xT_v[:, :, c0:c0 + W], writes=[xk])
            at, ak = ar.next()
            half = nA // 2
            P.dma(at[:, :half, :], aT_v[:, :half, c0:c0 + W], writes=[ak + "a"], eng="pool")
            P.dma(at[:, half:, :], aT_v[:, half:, c0:c0 + W], writes=[ak + "b"])
            aks = [ak + "a", ak + "b"]
            cat, ck = catr.next()
            if layer == 0:
                for i in range(4):
                    P.op("pool", lambda e, i=i: e.tensor_copy(out=cat[:, i, :], in_=at[:, i, :]), aks, [ck + str(i)])
                for h in range(4):
                    dt_, dk_ = tmpr.next()
                    P.op("dve", lambda e, h=h, dt_=dt_: e.scalar_tensor_tensor(out=dt_[:, :], in0=at[:, 8 + h, :], scalar=neg_lam[:, 0:1], in1=at[:, 4 + h, :], op0=ALU.mult, op1=ALU.add), aks + ["neg_lam"], [dk_])
                    rms_T(C, "sub", [dt_[:, :]], [dk_], [gains[:, 0:1]], 128, [cat[:, 4 + h, :]], [ck + str(4 + h)], W, ones, pss, sqr, rsr, mul=1.0 - LAM_INIT0)
            else:
                for c2 in range(2):
                    ns, nk_ = tmpr.next()
                    ds, dk_ = tmpr.next()
                    P.op("pool", lambda e, c2=c2, ns=ns: e.tensor_tensor(out=ns[:, :], in0=at[:, c2, :], in1=at[:, 2 + c2, :], op=ALU.add), aks, [nk_])
                    P.op("pool", lambda e, c2=c2, ns=ns: e.tensor_tensor(out=ns[:, :], in0=ns[:, :], in1=at[:, 4 + c2, :], op=ALU.add), aks + [nk_], [nk_])
                    P.op("pool", lambda e, c2=c2, ds=ds: e.tensor_tensor(out=ds[:, :], in0=at[:, 6 + c2, :], in1=at[:, 8 + c2, :], op=ALU.add), aks, [dk_])
                    P.op("pool", lambda e, c2=c2, ds=ds: e.tensor_tensor(out=ds[:, :], in0=ds[:, :], in1=at[:, 10 + c2, :], op=ALU.add), aks + [dk_], [dk_])
                    P.op("dve", lambda e, ds=ds: e.reciprocal(out=ds[:, :], in_=ds[:, :]), [dk_], [dk_])
                    P.op("dve", lambda e, c2=c2, ns=ns, ds=ds: e.tensor_tensor(out=cat[:, c2, :], in0=ns[:, :], in1=ds[:, :], op=ALU.mult), [nk_, dk_], [ck + str(c2)])
                for i in range(2):
                    P.op("pool", lambda e, i=i: e.tensor_copy(out=cat[:, 2 + i, :], in_=at[:, 12 + i, :]), aks, [ck + str(2 + i)])
            cks = [ck + str(i) for i in range(nK)]
            ht, hk = hr.next()
            for oc in range(8):
                ps, pk = psr.next()
                for kc in range(nK):
                    P.op("pe", lambda e, kc=kc, ps=ps, oc=oc: e.matmul(ps[:, :], lhsT=w_b[:, kc, oc * 128:(oc + 1) * 128], rhs=cat[:, kc, :], start=(kc == 0), stop=(kc == nK - 1)), cks + ["w_b"], [pk])
                C.evac(ht[:, oc, :], ps[:, :], [pk], [hk])
            rms_T(C, "post", [ht[:, kc, :] for kc in range(8)], [hk] * 8, [gains[:, 1 + kc:2 + kc] for kc in range(8)], 1024,
                  [ht[:, kc, :] for kc in range(8)], [hk + "n"] * 8, W, ones, pss, sqr, rsr)
            P.op("pool", lambda e: e.tensor_tensor(out=xt[:, :, :], in0=xt[:, :, :], in1=ht[:, :, :], op=ALU.add), [xk, hk + "n", hk], [xk, hk])
            P.dma(xm_v[:, :, c0:c0 + W], xt[:, :, :], reads=[xk])

        for tt in range(NT):
            tile_body(tt)
        P.emit(st)
    return nc


def build_ffn(T=2048):
    nc = bass.Bass("TRN2", target_bir_lowering=False)
    W = 256
    NT = T // W
    W2 = W + 2
    with contextlib.ExitStack() as st:
        C = Ctx(nc, st)
        P = C.P
        sb = C.sb
        xT = C.dram("xmT", [1024, T + 2], F32)
        w_up = C.dram("w_up", [1024, 5632], F32)
        w_dn = C.dram("w_dn", [2816, 1024], F32)
        cw_d = C.dram("cw", [128, 44, 3], F32)
        cb_d = C.dram("cb", [128, 44], F32)
        gains_d = C.dram("gains", [128, 16], F32)
        xo = C.dram("xoT", [1024, T], F32, "ExternalOutput")
        wu_b = sb("wu_b", [128, 8, 5632], BF16)
        wd_b = sb("wd_b", [128, 22, 1024], BF16)
        cw = sb("cw_t", [128, 44, 3], F32)
        cb = sb("cb_t", [128, 44], F32)
        gains = sb("gains_t", [128, 16], F32)
        C.eps_t = sb("eps_t", [128, 1], F32)
        P.op("pool", lambda e: e.memset(C.eps_t[:], EPS), [], ["eps_t"])
        ones = C.ones_f32()
        wu_v = w_up.rearrange("(kc p) n -> p kc n", p=128)
        wu_keys = []
        for kc in range(8):
            for hlf in range(2):
                k = f"wu{kc}_{hlf}"
                P.dma(wu_b[:, kc, hlf * 2816:(hlf + 1) * 2816], wu_v[:, kc, hlf * 2816:(hlf + 1) * 2816], writes=[k], eng="pool")
                wu_keys.append(k)
        wd_v = w_dn.rearrange("(kc p) n -> p kc n", p=128)
        wd_keys = []
        for i in range(0, 22, 2):
            k = f"wd{i}"
            P.dma(wd_b[:, i:i + 2, :], wd_v[:, i:i + 2, :], writes=[k], eng="pool")
            wd_keys.append(k)
        P.dma(cw[:], cw_d[:, :, :], writes=["cw"])
        P.dma(cb[:], cb_d[:, :], writes=["cb"])
        P.dma(gains[:], gains_d[:, :], writes=["gains"])
        xr = Ring(sb, "xt", 2, [128, 8, W2], F32)
        xnr = Ring(sb, "xn", 2, [128, 8, W2], BF16)
        gtr = Ring(sb, "gt", 1, [128, 22, W], BF16)
        hr = Ring(sb, "ht", 1, [128, 8, W], F32)
        psr = C.psum_ring(6)
        pss = Ring(C.psb, "pss", 2, [128, 512], F32)
        sqr = Ring(sb, "sq", 3, [128, W2], F32)
        rsr = Ring(sb, "rs", 2, [128, W2], F32)
        accr = Ring(sb, "acc", 6, [128, W], F32)
        sgr = Ring(sb, "sg", 3, [128, W], F32)
        xT_v = xT.rearrange("(kc p) t -> p kc t", p=128)
        xo_v = xo.rearrange("(kc p) t -> p kc t", p=128)

        def tile_body(tt):
            s0 = tt * W
            xt, xk = xr.next()
            P.dma(xt[:], xT_v[:, :, s0:s0 + W2], writes=[xk])
            xn, xnk = xnr.next()
            rms_T(C, "pre", [xt[:, kc, :] for kc in range(8)], [xk] * 8, [gains[:, kc:kc + 1] for kc in range(8)], 1024,
                  [xn[:, kc, :] for kc in range(8)], [xnk] * 8, W2, ones, pss, sqr, rsr)
            gt, gk = gtr.next()

            def ff_chunk(i):
                accs = []
                for col0, ch in ((128 * i, i), (2816 + 128 * i, 22 + i)):
                    ps, pk = psr.next()
                    for kc in range(8):
                        P.op("pe", lambda e, kc=kc, ps=ps, col0=col0: e.matmul(ps[:, :W2], lhsT=wu_b[:, kc, col0:col0 + 128], rhs=xn[:, kc, :], start=(kc == 0), stop=(kc == 7)), [xnk] + wu_keys, [pk])
                    acc, acck = accr.next()
                    P.op("dve", lambda e, ps=ps, acc=acc, ch=ch: e.tensor_scalar(out=acc[:, :], in0=ps[:, 2:W2], scalar1=cw[:, ch, 2:3], scalar2=cb[:, ch:ch + 1], op0=ALU.mult, op1=ALU.add), [pk, "cw", "cb"], [acck])
                    P.op("dve", lambda e, ps=ps, acc=acc, ch=ch: e.scalar_tensor_tensor(out=acc[:, :], in0=ps[:, 1:W2 - 1], scalar=cw[:, ch, 1:2], in1=acc[:, :], op0=ALU.mult, op1=ALU.add), [pk, "cw", acck], [acck])
                    P.op("dve", lambda e, ps=ps, acc=acc, ch=ch: e.scalar_tensor_tensor(out=acc[:, :], in0=ps[:, 0:W], scalar=cw[:, ch, 0:1], in1=acc[:, :], op0=ALU.mult, op1=ALU.add), [pk, "cw", acck], [acck])
                    accs.append((acc, acck))
                sg, sgk = sgr.next()
                P.op("act", lambda e: e.activation(out=sg[:, :], in_=accs[0][0][:, :], func=AF.Silu), [accs[0][1]], [sgk])
                P.op("pool", lambda e: e.tensor_tensor(out=gt[:, i, :], in0=sg[:, :], in1=accs[1][0][:, :], op=ALU.mult), [sgk, accs[1][1]], [gk + str(i)])

            for i in range(22):
                ff_chunk(i)
            gks = [gk + str(i) for i in range(22)]
            ht, hk = hr.next()
            for oc in range(8):
                ps, pk = psr.next()
                for i in range(22):
                    P.op("pe", lambda e, i=i, ps=ps, oc=oc: e.matmul(ps[:, :W], lhsT=wd_b[:, i, oc * 128:(oc + 1) * 128], rhs=gt[:, i, :], start=(i == 0), stop=(i == 21)), gks + wd_keys, [pk])
                C.evac(ht[:, oc, :], ps[:, :W], [pk], [hk])
            rms_T(C, "post", [ht[:, kc, :] for kc in range(8)], [hk] * 8, [gains[:, 8 + kc:9 + kc] for kc in range(8)], 1024,
                  [ht[:, kc, :] for kc in range(8)], [hk + "n"] * 8, W, ones, pss, sqr, rsr)
            P.op("pool", lambda e: e.tensor_tensor(out=ht[:, :, :], in0=xt[:, :, 2:W2], in1=ht[:, :, :], op=ALU.add), [xk, hk + "n", hk], [hk + "o"])
            P.dma(xo_v[:, :, s0:s0 + W], ht[:, :, :], reads=[hk + "o"], writes=[hk, hk + "n"] + gks)

        for tt in range(NT):
            tile_body(tt)
        P.emit(st)
    return nc


def build_pre1(T=2048):
    nc = bass.Bass("TRN2", target_bir_lowering=False)
    W = 512
    NT = T // W
    with contextlib.ExitStack() as st:
        C = Ctx(nc, st)
        P = C.P
        sb = C.sb
        xT = C.dram("xT", [1024, T], F32)
        w_in = C.dram("w_in", [1024, 3072], F32)
        gains_d = C.dram("gains", [128, 8], F32)
        fm_o = C.dram("fmT", [2048, T], BF16, "ExternalOutput")
        cv_o = C.dram("cv", [T, 768], BF16, "ExternalOutput")
        sv_o = C.dram("sv", [T, 256], BF16, "ExternalOutput")
        w_b = sb("w_b", [128, 8, 3072], BF16)
        gains = sb("gains_t", [128, 8], F32)
        C.eps_t = sb("eps_t", [128, 1], F32)
        P.op("pool", lambda e: e.memset(C.eps_t[:], EPS), [], ["eps_t"])
        ones = C.ones_f32()
        w_v = w_in.rearrange("(kc p) n -> p kc n", p=128)
        wkeys = []
        for kc in range(8):
            P.dma(w_b[:, kc, :], w_v[:, kc, :], writes=[f"w{kc}"], eng="pool")
            wkeys.append(f"w{kc}")
        P.dma(gains[:], gains_d[:, :], writes=["gains"])
        xr = Ring(sb, "xt", 2, [128, 8, W], F32)
        hnr = Ring(sb, "hn", 2, [128, 8, W], BF16)
        psr = C.psum_ring(6)
        pss = Ring(C.psb, "pss", 2, [128, 512], F32)
        sqr = Ring(sb, "sq", 3, [128, W], F32)
        rsr = Ring(sb, "rs", 2, [128, W], F32)
        stg = Ring(sb, "stg", 6, [128, W], BF16)
        xT_v = xT.rearrange("(kc p) t -> p kc t", p=128)
        fm_cols = [128 * i for i in range(12)] + [2304, 2432, 2560, 2688]

        def tile_body(tt):
            c0 = tt * W
            xt, xk = xr.next()
            P.dma(xt[:], xT_v[:, :, c0:c0 + W], writes=[xk])
            hn, hk = hnr.next()
            rms_T(C, "pre", [xt[:, kc, :] for kc in range(8)], [xk] * 8, [gains[:, kc:kc + 1] for kc in range(8)], 1024,
                  [hn[:, kc, :] for kc in range(8)], [hk] * 8, W, ones, pss, sqr, rsr)
            for oi, col in enumerate(fm_cols):
                ps, pk = psr.next()
                for kc in range(8):
                    P.op("pe", lambda e, kc=kc, ps=ps, col=col: e.matmul(ps[:, :], lhsT=w_b[:, kc, col:col + 128], rhs=hn[:, kc, :], start=(kc == 0), stop=(kc == 7)), [hk] + wkeys, [pk])
                so, sk = stg.next()
                C.evac(so[:, :], ps[:, :], [pk], [sk])
                P.dma(fm_o[128 * oi:128 * oi + 128, c0:c0 + W], so[:, :], reads=[sk])
            for tb in range(W // 128):
                for col, n, dst, dc in ((1536, 512, cv_o, 0), (2048, 256, cv_o, 512), (2816, 256, sv_o, 0)):
                    ps, pk = psr.next()
                    for kc in range(8):
                        P.op("pe", lambda e, kc=kc, ps=ps, col=col, n=n, tb=tb: e.matmul(ps[:, :n], lhsT=hn[:, kc, tb * 128:(tb + 1) * 128], rhs=w_b[:, kc, col:col + n], start=(kc == 0), stop=(kc == 7)), [hk] + wkeys, [pk])
                    so, sk = stg.next()
                    C.evac(so[:, :n], ps[:, :n], [pk], [sk])
                    P.dma(dst[c0 + tb * 128:c0 + tb * 128 + 128, dc:dc + n], so[:, :n], reads=[sk])

        for tt in range(NT):
            tile_body(tt)
        P.emit(st)
    return nc


def rope_tables(pos):
    half = 16
    inv = (np.float32(10000.0) ** (-(np.arange(half, dtype=np.float32)) / np.float32(half))).astype(np.float32)
    ang = pos.astype(np.float32)[:, None] * inv[None, :]
    cos, sin = np.cos(ang).astype(np.float32), np.sin(ang).astype(np.float32)
    T = pos.shape[0]
    t = np.zeros((128, 2, T), np.float32)
    t[64:80, 0] = cos.T; t[80:96, 0] = cos.T
    t[64:80, 1] = -sin.T; t[80:96, 1] = sin.T
    return t

def prep_A(inp):
    x = inp["x"][0]
    xT = np.ascontiguousarray(x.T)
    w_in = np.ascontiguousarray(inp["ev_w_in"][0])
    z = np.zeros((1024, 64), np.float32)
    kr = w_in[:, 384:416]
    w_kr = np.stack([np.concatenate([z, kr], 1), np.concatenate([z, kr[:, 16:], kr[:, :16]], 1)], 1)
    uq = inp["ev_w_uq"][0].reshape(256, 8, 96)
    uqs = np.concatenate([uq[:, :, :64], uq[:, :, 80:96], uq[:, :, 64:80]], 2)
    w_uq = np.ascontiguousarray(np.stack([uq, uqs], 1))
    ukv = inp["ev_w_ukv"][0].reshape(128, 8, 128)
    w_uk = np.ascontiguousarray(ukv[:, :, :64].reshape(128, 512))
    w_uv = np.ascontiguousarray(ukv[:, :, 64:].reshape(128, 512))
    gains = np.concatenate([inp["ev_pre_g"][0].reshape(8, 128).T, inp["ev_cq_g"][0].reshape(2, 128).T, inp["ev_ckv_g"][0].reshape(1, 128).T], 1)
    gains = np.ascontiguousarray(gains.astype(np.float32))
    maps = []
    for c in range(8):
        pos = np.arange(2048 * c, 2048 * c + 2048)
        maps.append(dict(xT=np.ascontiguousarray(xT[:, 2048 * c:2048 * c + 2048]), w_in=w_in, w_kr=np.ascontiguousarray(w_kr), w_uq=w_uq,
                         w_uk=w_uk, w_uv=w_uv, gains=gains, rope_t=rope_tables(pos)))
    return maps

def bf(a):
    return np.ascontiguousarray(a).astype(NPBF)

def consts_B():
    k = np.arange(128)[:, None]; q = np.arange(128)[None, :]
    maskb = np.where(k <= q, 0.0, -30000.0).astype(NPBF)
    ident = np.eye(128, dtype=np.float32).astype(NPBF)
    return maskb, ident

def v_layout(v, S):
    d = v.shape[1]
    o = np.ones((128, S // 128, d + 1), NPBF)
    o[:, :, :d] = v.reshape(S // 128, 128, d).transpose(1, 0, 2)
    return o

def prep_B(A, S):
    maskb, ident = consts_B()
    t = np.arange(S)
    maps = []
    for c in range(8):
        h, j = c // 2, c % 2
        slope = 2.0 ** (-2.0 * (h + 1))
        qa = np.stack([-slope * 128 * (t // 128) * 8, -slope * (t % 128) * 8, np.ones(S), np.ones(S)]).astype(NPBF)
        ka = np.stack([np.ones(S), np.ones(S), slope * 128 * (t // 128) * 8, slope * (t % 128) * 8]).astype(NPBF)
        r0 = h * 128 + j * 64
        maps.append(dict(
            QT=bf(A["qT"][c]), KT=bf(np.concatenate([A["knT"][c * 64:(c + 1) * 64], A["krT"]], 0)),
            V1=v_layout(A["vm"][:, c * 64:(c + 1) * 64], S),
            dQT=bf(np.concatenate([A["dqT"][r0:r0 + 64], qa], 0)), dKT=bf(np.concatenate([A["dkT"][r0:r0 + 64], ka], 0)),
            dV1=v_layout(A["dv"][:, h * 128:(h + 1) * 128], S), maskb=maskb, ident=ident))
    return maps

DIL = ((128, 1), (512, 4), (2048, 16))

def consts_D(half):
    k = np.arange(128)[:, None, None]; j = np.arange(8)[None, :, None]; q = np.arange(512)[None, None, :]
    valid = (128 * j + k) < (512 * half + q)
    M01 = valid.astype(np.float32).astype(NPBF)
    Mneg = np.where(valid, 0.0, -30000.0).astype(NPBF)
    jj = np.arange(128)[:, None]; kk = np.arange(128)[None, :]
    negtri = np.where(jj >= kk, -8.0, 0.0).astype(NPBF)
    return M01, Mneg, negtri

def dil_bias():
    k = np.arange(128)[:, None].astype(np.float64); q = np.arange(128)[None, :].astype(np.float64)
    Bt = np.zeros((128, 12, 2, 128), np.float32)
    for n in range(12):
        g = n // 4
        d = DIL[g][1]
        slope = 2.0 ** (-8.0 * (n + 1) / 12)
        Bt[:, n, 0, :] = np.where(k >= q, -slope * d * (q + 128 - k) * 8, -240000.0)
        Bt[:, n, 1, :] = np.where(k <= q, -slope * d * (q - k) * 8, -240000.0)
    return Bt

def dil_perm(S, d):
    L = S // d
    pi = np.arange(S)
    return d * (pi % L) + (pi // L)

def prep_D(L1, S, n_slots=16):
    _, ident = consts_B()
    ident_f = np.eye(128, dtype=np.float32)
    Bt = dil_bias()
    S_sb = n_slots * 1024
    maps = []
    perms = [dil_perm(S, d) for _, d in DIL]
    for c in range(8):
        h, half = c // 2, c % 2
        M01, Mneg, negtri = consts_D(half)
        qcols = np.concatenate([np.arange(512 * (2 * i + half), 512 * (2 * i + half) + 512) for i in range(n_slots)])
        sQT = bf(L1["sqT"][h * 64:(h + 1) * 64][:, qcols])
        sKT = bf(L1["skT"][h * 64:(h + 1) * 64][:, :S_sb])
        sV = bf(L1["sv"][:S_sb, h * 64:(h + 1) * 64].reshape(S_sb // 128, 128, 64).transpose(1, 0, 2))
        dlQ = np.zeros((12, 65, 2048), NPBF); dlK = np.zeros((12, 65, 2176), NPBF); dlV = np.zeros((12, 128, 17, 128), NPBF)
        for n in range(12):
            g = n // 4
            d = DIL[g][1]; Lc = S // d
            tp = perms[g]
            pi = np.arange(2048 * c, 2048 * c + 2048)
            dlQ[n, :64] = L1["cqT"][n * 64:(n + 1) * 64][:, tp[pi]]
            dlQ[n, 64] = np.where((pi % Lc) < 128, -30000.0, 0.0).astype(NPBF)
            pk = np.arange(2048 * c - 128, 2048 * c + 2048)
            ok = pk >= 0
            tk = tp[np.maximum(pk, 0)]
            kk = np.asarray(L1["ckT"][n * 64:(n + 1) * 64][:, tk]).copy(); kk[:, ~ok] = 0
            dlK[n, :64] = kk; dlK[n, 64] = 1.0
            vv = np.asarray(L1["cv"][tk, n * 64:(n + 1) * 64]).copy(); vv[~ok] = 0
            dlV[n, :, :, :64] = vv.reshape(17, 128, 64).transpose(1, 0, 2); dlV[n, :, :, 64:] = 1.0
        maps.append(dict(sQT=sQT, sKT=sKT, sV=sV, M01=M01, Mneg=Mneg, negtri=negtri, ident=ident, ident_f=ident_f,
                         dlQ=dlQ, dlK=dlK, dlV=dlV, dlB=Bt))
    return maps

def post_D(res, S, n_slots=16):
    S_sb = n_slots * 1024
    o_sb = np.zeros((S_sb, 256), np.float32)
    for c in range(8):
        h, half = c // 2, c % 2
        o = np.asarray(res[c]["o_sb"]).reshape(n_slots, 512, 64)
        for i in range(n_slots):
            o_sb[512 * (2 * i + half):512 * (2 * i + half) + 512, h * 64:(h + 1) * 64] = o[i]
    odl = np.concatenate([np.asarray(res[c]["o_dl"]).reshape(12, 2048, 128) for c in range(8)], 1)
    num = [np.zeros((S, 256), np.float32) for _ in range(3)]; den = [np.zeros((S, 256), np.float32) for _ in range(3)]
    for n in range(12):
        g, j = n // 4, n % 4
        tp = dil_perm(S, DIL[g][1])
        num[g][tp, j * 64:(j + 1) * 64] = odl[n, :, :64]
        den[g][tp, j * 64:(j + 1) * 64] = odl[n, :, 64:]
    return o_sb, num, den


def _run(nc, maps):
    res = run_bass_kernel_spmd(nc, maps, core_ids=list(range(8)))
    return res.results


def _chunks(g, n):
    return np.ascontiguousarray(np.asarray(g, np.float32).reshape(n, 128).T)


def kernel(**inp):
    inp = {k: np.asarray(v) for k, v in inp.items()}
    S = 16384
    T = 2048
    f32 = np.float32
    RA = _run(build_A(), prep_A(inp))
    A = {}
    for k, ax in (("qT", 2), ("krT", 1), ("knT", 1), ("vm", 0), ("dqT", 1), ("dkT", 1), ("dv", 0)):
        A[k] = np.concatenate([np.asarray(RA[c][k]) for c in range(8)], ax)
    RB = _run(build_B(), prep_B(A, S))
    aT = np.concatenate([np.asarray(RB[h]["o_mla"]).T for h in range(8)]
                        + [np.asarray(RB[2 * h]["o_d"]).T for h in range(4)]
                        + [np.asarray(RB[2 * h + 1]["o_d"]).T for h in range(4)], 0).astype(f32)
    xT = np.ascontiguousarray(inp["x"][0].T)
    out = None
    for layer in range(2):
        if layer == 0:
            w_out = np.ascontiguousarray(inp["ev_w_out"][0])
            gains = np.concatenate([_chunks(inp["ev_subln_g"][0], 1), _chunks(inp["ev_post_g"][0], 8)], 1)
            lamv = np.stack([inp["ev_lam_q1"][0], inp["ev_lam_k1"][0], inp["ev_lam_q2"][0], inp["ev_lam_k2"][0]], 0).astype(f32)
        else:
            w_out = np.ascontiguousarray(inp["od_w_out"][0])
            gains = np.concatenate([np.ones((128, 1), f32), _chunks(inp["od_post_g"][0], 8)], 1)
            lamv = np.zeros((4, 64), f32)
        lamv = np.ascontiguousarray(np.broadcast_to(lamv[None], (128, 4, 64)))
        gains = np.ascontiguousarray(gains.astype(f32))
        maps = [dict(xT=np.ascontiguousarray(xT[:, T * c:T * c + T]), aT=np.ascontiguousarray(aT[:, T * c:T * c + T]),
                     w_out=w_out, gains=gains, lamv=lamv) for c in range(8)]
        RC = _run(build_post(layer), maps)
        xmT = np.concatenate([np.asarray(RC[c]["xmT"]) for c in range(8)], 1)
        xh = np.concatenate([np.zeros((1024, 2), f32), xmT], 1)
        cw = np.ascontiguousarray(inp["ffn_conv_w"][layer].reshape(3, 44, 128).transpose(2, 1, 0)).astype(f32)
        cb = _chunks(inp["ffn_conv_b"][layer], 44)
        gains = np.ascontiguousarray(np.concatenate([_chunks(inp["ffn_pre_g"][layer], 8), _chunks(inp["ffn_post_g"][layer], 8)], 1))
        w_up = np.ascontiguousarray(inp["ffn_w_up"][layer]); w_dn = np.ascontiguousarray(inp["ffn_w_down"][layer])
        maps = [dict(xmT=np.ascontiguousarray(xh[:, T * c:T * c + T + 2]), w_up=w_up, w_dn=w_dn, cw=cw, cb=cb, gains=gains) for c in range(8)]
        RF = _run(build_ffn(), maps)
        xT = np.concatenate([np.asarray(RF[c]["xoT"]) for c in range(8)], 1)
        if layer == 0:
            maps = [dict(xT=np.ascontiguousarray(xT[:, T * c:T * c + T]), w_in=np.ascontiguousarray(inp["od_w_in"][0]),
                         gains=_chunks(inp["od_pre_g"][0], 8)) for c in range(8)]
            RP = _run(build_pre1(), maps)
            fm = np.concatenate([np.asarray(RP[c]["fmT"]) for c in range(8)], 1)
            L1 = dict(cqT=fm[0:768], ckT=fm[768:1536], sqT=fm[1536:1792], skT=fm[1792:2048],
                      cv=np.concatenate([np.asarray(RP[c]["cv"]) for c in range(8)], 0),
                      sv=np.concatenate([np.asarray(RP[c]["sv"]) for c in range(8)], 0))
            RD = _run(build_D(), prep_D(L1, S))
            o_sb, num, den = post_D(RD, S)
            aT = np.concatenate([n.T for n in num] + [d.T for d in den] + [o_sb.T], 0).astype(f32)
    return np.ascontiguousarray(xT.T)[None].astype(f32)
```

```python
import contextlib
import numpy as np
import ml_dtypes
import concourse.bass as bass
import concourse.mybir as mybir
from concourse.bass_utils import run_bass_kernel_spmd

F32 = mybir.dt.float32
BF16 = mybir.dt.bfloat16
AF = mybir.ActivationFunctionType
ALU = mybir.AluOpType
AX = mybir.AxisListType
NPBF = ml_dtypes.bfloat16

EPOCH = 30000
N_DMA_SEMS = 48


class Op:
    __slots__ = ("eng", "fn", "deps", "is_dma", "idx", "signal", "sem", "val", "dsem_slot")

    def __init__(self, eng, fn, is_dma):
        self.eng = eng
        self.fn = fn
        self.deps = []
        self.is_dma = is_dma
        self.signal = False
        self.sem = None
        self.val = None


class Prog:
    ENGS = ("pe", "act", "dve", "pool", "sp")

    def __init__(self, nc):
        self.nc = nc
        self.ops = {e: [] for e in self.ENGS}
        self.buf = {}
        self.dma_count = 0
        self.dma_last = {}
        self.all_ops = []

    def op(self, eng, fn, reads=(), writes=(), dma=False, extra=()):
        o = Op(eng, fn, dma)
        deps = set()
        for k in reads:
            st = self.buf.get(k)
            if st is not None and st[0] is not None:
                deps.add(st[0])
        for k in writes:
            st = self.buf.get(k)
            if st is not None:
                if st[0] is not None:
                    deps.add(st[0])
                for r in st[1]:
                    deps.add(r)
        for d in extra:
            if d is not None:
                deps.add(d)
        if dma:
            slot = self.dma_count % N_DMA_SEMS
            self.dma_count += 1
            o.dsem_slot = slot
            prev = self.dma_last.get(slot)
            if prev is not None:
                deps.add(prev)
            self.dma_last[slot] = o
        deps.discard(o)
        best = {}
        keep = []
        for d in deps:
            if d.is_dma:
                keep.append(d)
            else:
                b = best.get(d.eng)
                if b is None or d.idx > b.idx:
                    best[d.eng] = d
        o.deps = keep + list(best.values())
        o.idx = len(self.ops[eng])
        for k in reads:
            st = self.buf.setdefault(k, [None, []])
            st[1].append(o)
        for k in writes:
            self.buf[k] = [o, []]
        self.ops[eng].append(o)
        self.all_ops.append(o)
        return o

    def dma(self, out, in_, reads=(), writes=(), eng="sp", **kw):
        return self.op(eng, lambda e: e.dma_start(out=out, in_=in_, **kw), reads, writes, dma=True)

    def emit(self, stack):
        nc = self.nc
        for o in self.all_ops:
            for d in o.deps:
                if d.is_dma:
                    d.signal = True
                elif d.eng != o.eng or o.eng != "pe" or o.is_dma:
                    d.signal = True
        n_sig = {e: 0 for e in self.ENGS}
        for e in self.ENGS:
            for o in self.ops[e]:
                if o.is_dma:
                    o.signal = True
                elif o.signal:
                    n_sig[e] += 1
        sems = {}
        for e in self.ENGS:
            n_ep = (n_sig[e] + EPOCH - 1) // EPOCH
            sems[e] = [stack.enter_context(nc.semaphore(f"s_{e}_{i}")) for i in range(max(n_ep, 1))]
        dsems = [stack.enter_context(nc.semaphore(f"s_dma_{i}")) for i in range(min(N_DMA_SEMS, max(self.dma_count, 1)))]
        dcount = [0] * N_DMA_SEMS
        for e in self.ENGS:
            c = 0
            for o in self.ops[e]:
                if o.is_dma:
                    continue
                if o.signal:
                    o.sem = sems[e][c // EPOCH]
                    o.val = c % EPOCH + 1
                    c += 1
        for o in self.all_ops:
            if o.is_dma:
                dcount[o.dsem_slot] += 16
                o.sem = dsems[o.dsem_slot]
                o.val = dcount[o.dsem_slot]
        block = stack.enter_context(nc.Block())
        engmap = {"pe": "tensor", "act": "scalar", "dve": "vector", "pool": "gpsimd", "sp": "sync"}
        last_dmas = [o for o in self.all_ops if o.is_dma]

        def make_section(e):
            ops = self.ops[e]

            def section(engine):
                waited = {}
                for o in ops:
                    for d in o.deps:
                        if (not d.is_dma) and d.eng == e and e == "pe" and not o.is_dma:
                            continue
                        if d.is_dma:
                            key = ("d", d.dsem_slot)
                            v = d.val
                        else:
                            key = (d.eng, id(d.sem))
                            v = d.val
                        if waited.get(key, 0) >= v:
                            continue
                        engine.wait_ge(d.sem, v)
                        waited[key] = v
                    ins = o.fn(engine)
                    if o.signal:
                        ins.then_inc(o.sem, 16 if o.is_dma else 1)
                if e == "sp":
                    for slot in range(len(dsems)):
                        if dcount[slot] > 0:
                            engine.wait_ge(dsems[slot], dcount[slot])
            return section

        for e in self.ENGS:
            if not self.ops[e] and e != "sp":
                continue
            getattr(block, engmap[e])(make_section(e))


EPS = 1e-6


class Ring:
    def __init__(self, alloc, name, n, shape, dtype):
        self.t = [alloc(f"{name}{i}", shape, dtype) for i in range(n)]
        self.k = [f"{name}{i}" for i in range(n)]
        self.i = 0
        self.n = n

    def next(self):
        j = self.i % self.n
        self.i += 1
        return self.t[j], self.k[j]


class Ctx:
    def __init__(self, nc, st):
        self.nc = nc
        self.st = st
        self.P = Prog(nc)
        self.sb = lambda n, s, d: st.enter_context(nc.sbuf_tensor(n, s, d))
        self.psb = lambda n, s, d: st.enter_context(nc.psum_tensor(n, s, d))
        self.evac_i = 0

    def dram(self, n, s, dt, kind="ExternalInput"):
        return self.nc.dram_tensor(n, s, dt, kind=kind).ap()

    def psum_ring(self, n, name="ps"):
        return Ring(self.psb, name, n, [128, 512], F32)

    def ones_f32(self):
        P = self.P
        t = self.sb("ones_f", [128, 128], F32)
        P.op("pool", lambda e: e.memset(t[:], 1.0), [], ["ones_f"])
        return t

    def evac(self, out, in_, reads, writes):
        self.evac_i += 1
        if self.evac_i % 2:
            return self.P.op("act", lambda e: e.activation(out=out, in_=in_, func=AF.Copy), reads, writes)
        return self.P.op("dve", lambda e: e.tensor_copy(out=out, in_=in_), reads, writes)


def rms_T(C, tag, srcs, src_keys, gains, D, outs, out_keys, W, ones, ps_ring, sq_ring, rs_ring, mul=None):
    P = C.P
    n = len(srcs)
    ps, psk = ps_ring.next()
    for i in range(n):
        sq, sqk = sq_ring.next()
        rows = srcs[i].shape[0]
        P.op("act", lambda e, sq=sq, i=i, rows=rows: e.activation(out=sq[:rows, :W], in_=srcs[i], func=AF.Square), [src_keys[i]], [sqk])
        P.op("pe", lambda e, sq=sq, i=i, rows=rows, ps=ps: e.matmul(ps[:, :W], lhsT=ones[:rows, :], rhs=sq[:rows, :W], start=(i == 0), stop=(i == n - 1)), [sqk, "ones_f"], [psk])
    rs, rsk = rs_ring.next()
    P.op("act", lambda e: e.activation(out=rs[:, :W], in_=ps[:, :W], func=AF.Sqrt, scale=1.0 / D, bias=C.eps_t[:, 0:1]), [psk, "eps_t"], [rsk])
    P.op("dve", lambda e: e.reciprocal(out=rs[:, :W], in_=rs[:, :W]), [rsk], [rsk])
    if mul is not None:
        P.op("dve", lambda e: e.tensor_scalar(out=rs[:, :W], in0=rs[:, :W], scalar1=float(mul), scalar2=None, op0=ALU.mult), [rsk], [rsk])
    for i in range(n):
        rows = srcs[i].shape[0]
        eng = "dve"
        P.op(eng, lambda e, i=i, rows=rows: e.scalar_tensor_tensor(out=outs[i], in0=srcs[i], scalar=gains[i], in1=rs[:rows, :W], op0=ALU.mult, op1=ALU.mult), [src_keys[i], rsk, "gains"], [out_keys[i]])


def build_A(T=2048):
    nc = bass.Bass("TRN2", target_bir_lowering=False)
    W = 512
    NT = T // W
    with contextlib.ExitStack() as st:
        C = Ctx(nc, st)
        P = C.P
        sb = C.sb
        xT = C.dram("xT", [1024, T], F32)
        w_in = C.dram("w_in", [1024, 1952], F32)
        w_kr = C.dram("w_kr", [1024, 2, 96], F32)
        w_uq = C.dram("w_uq", [256, 2, 8, 96], F32)
        w_uk = C.dram("w_uk", [128, 512], F32)
        w_uv = C.dram("w_uv", [128, 512], F32)
        gains_d = C.dram("gains", [128, 11], F32)
        rope_d = C.dram("rope_t", [128, 2, T], F32)
        qT_o = C.dram("qT", [8, 96, T], BF16, "ExternalOutput")
        krT_o = C.dram("krT", [32, T], BF16, "ExternalOutput")
        knT_o = C.dram("knT", [512, T], BF16, "ExternalOutput")
        vm_o = C.dram("vm", [T, 512], BF16, "ExternalOutput")
        dqT_o = C.dram("dqT", [512, T], BF16, "ExternalOutput")
        dkT_o = C.dram("dkT", [512, T], BF16, "ExternalOutput")
        dv_o = C.dram("dv", [T, 512], BF16, "ExternalOutput")

        w_in_b = sb("w_in_b", [128, 8, 1952], BF16)
        w_kr_b = sb("w_kr_b", [128, 8, 2, 96], BF16)
        w_uq_b = sb("w_uq_b", [128, 2, 2, 8, 96], BF16)
        w_uk_b = sb("w_uk_b", [128, 512], BF16)
        w_uv_b = sb("w_uv_b", [128, 512], BF16)
        gains = sb("gains_t", [128, 11], F32)
        rope_t = sb("rope_tt", [128, 2, T], F32)
        C.eps_t = sb("eps_t", [128, 1], F32)
        P.op("pool", lambda e: e.memset(C.eps_t[:], EPS), [], ["eps_t"])
        ones = C.ones_f32()
        w_in_v = w_in.rearrange("(kc p) n -> p kc n", p=128)
        for kc in range(8):
            P.dma(w_in_b[:, kc, :], w_in_v[:, kc, :], writes=[f"w_in{kc}"], eng="pool")
        P.dma(w_kr_b[:], w_kr.rearrange("(kc p) v n -> p kc v n", p=128), writes=["w_kr"], eng="pool")
        P.dma(w_uq_b[:], w_uq.rearrange("(kc p) v h n -> p kc v h n", p=128), writes=["w_uq"], eng="pool")
        P.dma(w_uk_b[:], w_uk[:, :], writes=["w_uk"], eng="pool")
        P.dma(w_uv_b[:], w_uv[:, :], writes=["w_uv"], eng="pool")
        P.dma(gains[:], gains_d[:, :], writes=["gains"])
        P.dma(rope_t[:], rope_d[:, :, :], writes=["rope"])
        w_in_keys = [f"w_in{kc}" for kc in range(8)]

        xr = Ring(sb, "xt", 2, [128, 8, W], F32)
        hnr = Ring(sb, "hn", 2, [128, 8, W], BF16)
        psr = C.psum_ring(6)
        pss = Ring(C.psb, "pss", 2, [128, 512], F32)
        sqr = Ring(sb, "sq", 3, [128, W], F32)
        rsr = Ring(sb, "rs", 2, [128, W], F32)
        stg = Ring(sb, "stg", 4, [128, W], BF16)
        cqr = Ring(sb, "cq", 2, [128, 2, W], F32)
        cqnr = Ring(sb, "cqn", 2, [128, 2, W], BF16)
        ckvr = Ring(sb, "ckv", 2, [128, W], F32)
        ckvnr = Ring(sb, "ckvn", 2, [128, W], BF16)
        tmpr = Ring(sb, "tmp", 4, [128, W], F32)
        qstg = Ring(sb, "qstg", 3, [128, W], BF16)
        xT_v = xT.rearrange("(kc p) t -> p kc t", p=128)

        def rope_evac(psA, psAk, psB, psBk, out_ap, out_key, c0):
            t1, t1k = tmpr.next()
            t2, t2k = tmpr.next()
            P.op("dve", lambda e: e.tensor_tensor(out=t1[64:96, :], in0=psA[64:96, :], in1=rope_t[64:96, 0, c0:c0 + W], op=ALU.mult), [psAk, "rope"], [t1k])
            P.op("dve", lambda e: e.tensor_tensor(out=t2[64:96, :], in0=psB[64:96, :], in1=rope_t[64:96, 1, c0:c0 + W], op=ALU.mult), [psBk, "rope"], [t2k])
            P.op("pool", lambda e: e.tensor_tensor(out=out_ap, in0=t1[64:96, :], in1=t2[64:96, :], op=ALU.add), [t1k, t2k], [out_key])

        def tile_body(tt):
            c0 = tt * W
            xt, xk = xr.next()
            P.dma(xt[:], xT_v[:, :, c0:c0 + W], writes=[xk])
            hn, hk = hnr.next()
            rms_T(C, "pre", [xt[:, kc, :] for kc in range(8)], [xk] * 8, [gains[:, kc:kc + 1] for kc in range(8)], 1024,
                  [hn[:, kc, :] for kc in range(8)], [hk] * 8, W, ones, pss, sqr, rsr)

            def proj(lhs_fn, rows, keys):
                ps, pk = psr.next()
                for kc in range(8):
                    P.op("pe", lambda e, kc=kc, ps=ps: e.matmul(ps[:rows, :], lhsT=lhs_fn(kc), rhs=hn[:, kc, :], start=(kc == 0), stop=(kc == 7)), [hk] + keys, [pk])
                return ps, pk

            cq, cqk = cqr.next()
            for i in range(2):
                ps, pk = proj(lambda kc, i=i: w_in_b[:, kc, 128 * i:128 * i + 128], 128, w_in_keys)
                C.evac(cq[:, i, :], ps[:, :], [pk], [cqk])
            ckv, ckvk = ckvr.next()
            ps, pk = proj(lambda kc: w_in_b[:, kc, 256:384], 128, w_in_keys)
            C.evac(ckv[:, :], ps[:, :], [pk], [ckvk])
            psA, pkA = proj(lambda kc: w_kr_b[:, kc, 0, :], 96, ["w_kr"])
            psB, pkB = proj(lambda kc: w_kr_b[:, kc, 1, :], 96, ["w_kr"])
            so, sk = stg.next()
            rope_evac(psA, pkA, psB, pkB, so[64:96, :], sk, c0)
            P.dma(krT_o[:, c0:c0 + W], so[64:96, :], reads=[sk])
            for base, dst in ((416, dqT_o), (928, dkT_o)):
                for i in range(4):
                    ps, pk = proj(lambda kc, i=i, base=base: w_in_b[:, kc, base + 128 * i:base + 128 * i + 128], 128, w_in_keys)
                    so, sk = stg.next()
                    C.evac(so[:, :], ps[:, :], [pk], [sk])
                    P.dma(dst[128 * i:128 * i + 128, c0:c0 + W], so[:, :], reads=[sk])
            for tb in range(W // 128):
                ps, pk = psr.next()
                for kc in range(8):
                    P.op("pe", lambda e, kc=kc, ps=ps, tb=tb: e.matmul(ps[:, :], lhsT=hn[:, kc, tb * 128:(tb + 1) * 128], rhs=w_in_b[:, kc, 1440:1952], start=(kc == 0), stop=(kc == 7)), [hk] + w_in_keys, [pk])
                so, sk = stg.next()
                C.evac(so[:, :], ps[:, :], [pk], [sk])
                P.dma(dv_o[c0 + tb * 128:c0 + tb * 128 + 128, :], so[:, :], reads=[sk])
            cqn, cqnk = cqnr.next()
            rms_T(C, "cq", [cq[:, i, :] for i in range(2)], [cqk] * 2, [gains[:, 8 + i:9 + i] for i in range(2)], 256,
                  [cqn[:, i, :] for i in range(2)], [cqnk] * 2, W, ones, pss, sqr, rsr)
            for h in range(8):
                pq = []
                for v in range(2):
                    ps, pk = psr.next()
                    for k2 in range(2):
                        P.op("pe", lambda e, k2=k2, ps=ps, v=v, h=h: e.matmul(ps[:96, :], lhsT=w_uq_b[:, k2, v, h, :], rhs=cqn[:, k2, :], start=(k2 == 0), stop=(k2 == 1)), [cqnk, "w_uq"], [pk])
                    pq.append((ps, pk))
                so, sk = qstg.next()
                C.evac(so[0:64, :], pq[0][0][0:64, :], [pq[0][1]], [sk + "a"])
                rope_evac(pq[0][0], pq[0][1], pq[1][0], pq[1][1], so[64:96, :], sk + "b", c0)
                P.dma(qT_o[h, :, c0:c0 + W], so[0:96, :], reads=[sk + "a", sk + "b"], writes=[sk])
            ckvn, ckvnk = ckvnr.next()
            rms_T(C, "ckv", [ckv[:, :]], [ckvk], [gains[:, 10:11]], 128, [ckvn[:, :]], [ckvnk], W, ones, pss, sqr, rsr)
            for i in range(4):
                ps, pk = psr.next()
                P.op("pe", lambda e, ps=ps, i=i: e.matmul(ps[:, :], lhsT=w_uk_b[:, 128 * i:128 * i + 128], rhs=ckvn[:, :], start=True, stop=True), [ckvnk, "w_uk"], [pk])
                so, sk = stg.next()
                C.evac(so[:, :], ps[:, :], [pk], [sk])
                P.dma(knT_o[128 * i:128 * i + 128, c0:c0 + W], so[:, :], reads=[sk])
            for tb in range(W // 128):
                ps, pk = psr.next()
                P.op("pe", lambda e, ps=ps, tb=tb: e.matmul(ps[:, :], lhsT=ckvn[:, tb * 128:(tb + 1) * 128], rhs=w_uv_b[:, :], start=True, stop=True), [ckvnk, "w_uv"], [pk])
                so, sk = stg.next()
                C.evac(so[:, :], ps[:, :], [pk], [sk])
                P.dma(vm_o[c0 + tb * 128:c0 + tb * 128 + 128, :], so[:, :], reads=[sk])
        for tt in range(NT):
            tile_body(tt)
        P.emit(st)
    return nc


S_LEN = 16384


def attn_stream(C, tag, QT, KT, V, rows, dv, scale, out_d, psS, psO, ptr, ostg, rdr, ident, maskb, n_qt=32, q_off=0):
    P = C.P
    dv1 = dv + 1
    per_bank = 512 // dv1 if dv1 * 4 <= 512 else (2 if dv1 * 2 <= 512 else 1)
    per_bank = min(per_bank, 4)
    nbank = (4 + per_bank - 1) // per_bank
    out_v = out_d.rearrange("(t qs p) d -> t p qs d", qs=4, p=128)

    tiles = {}

    def get_tile(qt):
        if qt not in tiles:
            tiles[qt] = [psO.next() for _ in range(nbank)]
        return tiles[qt]

    def oslot(qt, qs):
        b, bk = get_tile(qt)[qs // per_bank]
        o = (qs % per_bank) * dv1
        return b[:, o:o + dv1], bk

    pend = {}

    def emit_qk(qt, kb):
        get_tile(qt)
        q0 = 512 * qt
        j = kb - 4 * qt
        diag = j >= 0
        c0 = 128 * j if diag else 0
        s, sk = psS.next()
        ksl = KT[:rows, kb * 128:(kb + 1) * 128]
        if diag:
            P.op("pe", lambda e: e.matmul(s[:, c0:c0 + 128], lhsT=ksl, rhs=QT[:rows, q0 + c0:q0 + c0 + 128], start=True, stop=False), ["QT", "KT"], [sk])
            P.op("pe", lambda e: e.matmul(s[:, c0:c0 + 128], lhsT=ident[:, :], rhs=maskb[:, :], start=False, stop=True), ["ident", "maskb"], [sk])
            if c0 + 128 < 512:
                P.op("pe", lambda e: e.matmul(s[:, c0 + 128:512], lhsT=ksl, rhs=QT[:rows, q0 + c0 + 128:q0 + 512], start=True, stop=True), ["QT", "KT"], [sk])
        else:
            P.op("pe", lambda e: e.matmul(s[:, :], lhsT=ksl, rhs=QT[:rows, q0:q0 + 512], start=True, stop=True), ["QT", "KT"], [sk])
        pt, ptk = ptr.next()
        P.op("act", lambda e: e.activation(out=pt[:, c0:512], in_=s[:, c0:512], func=AF.Exp, scale=scale), [sk], [ptk])
        pend[(qt, kb)] = (pt, ptk, c0)

    def emit_pv(qt, kb):
        pt, ptk, c0 = pend.pop((qt, kb))
        for qs in range(c0 // 128, 4):
            o, ok = oslot(qt, qs)
            P.op("pe", lambda e, o=o, qs=qs: e.matmul(o, lhsT=pt[:, qs * 128:(qs + 1) * 128], rhs=V[:, kb, :dv1], start=(kb == 0 and qs % per_bank == 0), stop=(kb == 4 * qt + qs)), [ptk, "V"], [ok])
        if kb == 4 * qt + 3:
            finish(qt)

    def finish(qt):
        og, ogk = ostg.next()
        rd, rdk = rdr.next()
        for qs in range(4):
            o, ok = oslot(qt, qs)
            P.op("dve", lambda e, o=o, qs=qs: e.reciprocal(out=rd[:, qs:qs + 1], in_=o[:, dv:dv1]), [ok], [rdk + str(qs)])
            P.op("dve", lambda e, o=o, qs=qs: e.tensor_scalar(out=og[:, qs, :dv], in0=o[:, :dv], scalar1=rd[:, qs:qs + 1], scalar2=None, op0=ALU.mult), [ok, rdk + str(qs)], [ogk + str(qs)])
        P.dma(out_v[qt + q_off], og[:, :, :dv], reads=[ogk + str(qs) for qs in range(4)] + [rdk + str(qs) for qs in range(4)], writes=[ogk, rdk])
        del tiles[qt]

    steps = [(qt, kb) for qt in range(n_qt) for kb in range(4 * qt + 4)]
    for n in range(len(steps) + 1):
        if n < len(steps):
            emit_qk(*steps[n])
        if n >= 1:
            emit_pv(*steps[n - 1])


def load_big(C, dst, src, rows, key, ncols, eng_cycle=("sp", "pool"), nsplit=4):
    step = ncols // nsplit
    for i in range(nsplit):
        C.P.dma(dst[:rows, i * step:(i + 1) * step], src[:, i * step:(i + 1) * step], writes=[key], eng=eng_cycle[i % len(eng_cycle)])


def build_B(n_qt=32):
    nc = bass.Bass("TRN2", target_bir_lowering=False)
    S = n_qt * 512
    NKB = S // 128
    with contextlib.ExitStack() as st:
        C = Ctx(nc, st)
        P = C.P
        sb = C.sb
        QT_d = C.dram("QT", [96, S], BF16)
        KT_d = C.dram("KT", [96, S], BF16)
        V_d = C.dram("V1", [128, NKB, 65], BF16)
        dQT_d = C.dram("dQT", [68, S], BF16)
        dKT_d = C.dram("dKT", [68, S], BF16)
        dV_d = C.dram("dV1", [128, NKB, 129], BF16)
        mask_d = C.dram("maskb", [128, 128], BF16)
        id_d = C.dram("ident", [128, 128], BF16)
        om_o = C.dram("o_mla", [S, 64], F32, "ExternalOutput")
        od_o = C.dram("o_d", [S, 128], F32, "ExternalOutput")
        QT = sb("QT_s", [128, S], BF16)
        KT = sb("KT_s", [128, S], BF16)
        Vf = sb("V_s", [128, NKB * 129], BF16)
        V = Vf[:, :].rearrange("p (k d) -> p k d", d=129)
        Vm = Vf[:, :NKB * 65].rearrange("p (k d) -> p k d", d=65)
        ident = sb("ident_s", [128, 128], BF16)
        maskb = sb("maskb_s", [128, 128], BF16)
        P.dma(ident[:], id_d[:, :], writes=["ident"])
        P.dma(maskb[:], mask_d[:, :], writes=["maskb"])
        psS = Ring(C.psb, "psS", 3, [128, 512], F32)
        psO = Ring(C.psb, "psO", 4, [128, 512], F32)
        ptr = Ring(sb, "pt", 3, [128, 512], BF16)
        ostg = Ring(sb, "og", 2, [128, 4, 128], F32)
        rdr = Ring(sb, "rd", 2, [128, 4], F32)
        load_big(C, QT, QT_d, 96, "QT", S)
        load_big(C, KT, KT_d, 96, "KT", S)
        P.dma(Vm, V_d[:, :, :], writes=["V"], eng="pool")
        attn_stream(C, "mla", QT, KT, Vm, 96, 64, 96 ** -0.5, om_o, psS, psO, ptr, ostg, rdr, ident, maskb, n_qt)
        load_big(C, QT, dQT_d, 68, "QT", S)
        load_big(C, KT, dKT_d, 68, "KT", S)
        P.dma(V[:, :, :], dV_d[:, :, :], writes=["V"], eng="pool")
        attn_stream(C, "diff", QT, KT, V, 68, 128, 0.125, od_o, psS, psO, ptr, ostg, rdr, ident, maskb, n_qt)
        P.emit(st)
    return nc


def sb_stream(C, QT, KT, V, M01, Mneg, negtri, negones, ident, one_t, out_d, psA, psB, psO, n_slots=16):
    P = C.P
    sb = C.sb
    e_r = Ring(sb, "sbe", 2, [128, 512], F32)
    sp_r = Ring(sb, "sbsp", 3, [128, 512], BF16)
    a_r = Ring(sb, "sba", 3, [128, 512], BF16)
    ra_r = Ring(sb, "sbra", 2, [128, 512], BF16)
    og_r = Ring(sb, "sbog", 2, [128, 4, 64], F32)
    out_v = out_d.rearrange("(t qs p) d -> t p qs d", qs=4, p=128)

    obs = {}
    ras = {}
    pend1 = {}
    pend2 = {}

    def stage1(i, kb, first, last):
        if i not in obs:
            obs[i] = psO.next()
        qsl = QT[:64, i * 512:(i + 1) * 512]
        j = kb - 8 * i
        diag = j >= 0
        ksl = KT[:64, kb * 128:(kb + 1) * 128]
        pa, pak = psA.next()
        P.op("pe", lambda e: e.matmul(pa[:, :], lhsT=ksl, rhs=qsl, start=True, stop=True), ["QT", "KT"], [pak])
        et, etk = e_r.next()
        P.op("act", lambda e: e.activation(out=et[:, :], in_=pa[:, :], func=AF.Exp, scale=0.125), [pak], [etk])
        sp, spk = sp_r.next()
        P.op("act", lambda e: e.activation(out=sp[:, :], in_=et[:, :], func=AF.Ln, scale=1.0, bias=one_t[:, 0:1]), [etk, "one_t"], [spk])
        if diag:
            P.op("dve", lambda e: e.tensor_tensor(out=sp[:, :], in0=sp[:, :], in1=M01[:, j, :], op=ALU.mult), [spk, "M01"], [spk])
        pend1[(i, kb)] = (sp, spk)

    def stage2(i, kb, first, last):
        sp, spk = pend1.pop((i, kb))
        qsl = QT[:64, i * 512:(i + 1) * 512]
        j = kb - 8 * i
        diag = j >= 0
        ksl = KT[:64, kb * 128:(kb + 1) * 128]
        pb, pbk = psB.next()
        ra = ras.get(i)
        nmm = 2 + (0 if first else 1) + (1 if diag else 0)
        cnt = [0]

        def mm(lhsT, rhs, reads):
            k = cnt[0]
            cnt[0] += 1
            P.op("pe", lambda e: e.matmul(pb[:, :], lhsT=lhsT, rhs=rhs, start=(k == 0), stop=(k == nmm - 1)), reads, [pbk])

        mm(ksl, qsl, ["QT", "KT"])
        mm(negtri[:, :], sp[:, :], [spk, "negtri"])
        if not first:
            mm(negones[:, :], ra[0][:, :], [ra[1], "negones"])
        if diag:
            mm(ident[:, :], Mneg[:, j, :], ["ident", "Mneg"])
        at, atk = a_r.next()
        P.op("act", lambda e: e.activation(out=at[:, :], in_=pb[:, :], func=AF.Exp, scale=0.125), [pbk], [atk])
        pend2[(i, kb)] = (at, atk)
        if not last:
            rn, rnk = ra_r.next()
            if first:
                P.op("pool", lambda e: e.tensor_copy(out=rn[:, :], in_=sp[:, :]), [spk], [rnk])
            else:
                P.op("pool", lambda e: e.tensor_tensor(out=rn[:, :], in0=ra[0][:, :], in1=sp[:, :], op=ALU.add), [ra[1], spk], [rnk])
            ras[i] = (rn, rnk)

    def stage3(i, kb, first, last):
        at, atk = pend2.pop((i, kb))
        ob, obk = obs[i]
        for qs in range(4):
            P.op("pe", lambda e, qs=qs: e.matmul(ob[:, qs * 64:(qs + 1) * 64], lhsT=at[:, qs * 128:(qs + 1) * 128], rhs=V[:, kb, :64], start=(first and qs == 0), stop=last), [atk, "V"], [obk])
        if last:
            og, ogk = og_r.next()
            C.evac(og[:, :, :], ob[:, :256].rearrange("p (a d) -> p a d", d=64), [obk], [ogk])
            P.dma(out_v[i], og[:, :, :], reads=[ogk])

    steps = []
    for i in range(n_slots):
        kbs = list(range(8 * i + 7, -1, -1))
        for n, kb in enumerate(kbs):
            steps.append((i, kb, n == 0, n == len(kbs) - 1))
    N = len(steps)
    for n in range(-1, N + 1):
        if 0 <= n + 1 < N:
            stage1(*steps[n + 1])
        if 0 <= n < N:
            stage2(*steps[n])
        if 0 <= n - 1 < N:
            stage3(*steps[n - 1])


def dil_part(C, dQ_d, dK_d, dV_d, Bt, ident_f, out_d, psS, psO, n_heads=12, n_blk=16):
    P = C.P
    sb = C.sb
    q_r = Ring(sb, "dlq", 2, [128, n_blk * 128], BF16)
    k_r = Ring(sb, "dlk", 2, [128, (n_blk + 1) * 128], BF16)
    v_r = Ring(sb, "dlv", 2, [128, n_blk + 1, 128], BF16)
    p_r = Ring(sb, "dlp", 3, [128, 256], BF16)
    og_r = Ring(sb, "dlo", 2, [128, n_blk, 128], F32)

    def head(n):
        q, qk = q_r.next()
        k, kk = k_r.next()
        v, vk = v_r.next()
        P.dma(q[:65, :], dQ_d[n], writes=[qk])
        P.dma(k[:65, :], dK_d[n], writes=[kk], eng="pool")
        P.dma(v[:, :, :], dV_d[n], writes=[vk])
        og, ogk = og_r.next()

        pend = {}

        def unit1(b):
            s, sk = psS.next()
            qs_ = q[:, b * 128:(b + 1) * 128]
            P.op("pe", lambda e: e.matmul(s[:, 0:128], lhsT=k[:65, b * 128:(b + 1) * 128], rhs=qs_[:65, :], start=True, stop=False), [qk, kk], [sk])
            P.op("pe", lambda e: e.matmul(s[:, 0:128], lhsT=ident_f[:, :], rhs=Bt[:, n, 0, :], start=False, stop=True), ["ident_f", "Bt"], [sk])
            P.op("pe", lambda e: e.matmul(s[:, 128:256], lhsT=k[:64, (b + 1) * 128:(b + 2) * 128], rhs=qs_[:64, :], start=True, stop=False), [qk, kk], [sk])
            P.op("pe", lambda e: e.matmul(s[:, 128:256], lhsT=ident_f[:, :], rhs=Bt[:, n, 1, :], start=False, stop=True), ["ident_f", "Bt"], [sk])
            p, pk = p_r.next()
            P.op("act", lambda e: e.activation(out=p[:, :], in_=s[:, 0:256], func=AF.Exp, scale=0.125), [sk], [pk])
            pend[b] = (p, pk)

        def unit2(b):
            p, pk = pend.pop(b)
            o, ok = psO.next()
            P.op("pe", lambda e: e.matmul(o[:, 0:128], lhsT=p[:, 0:128], rhs=v[:, b, :], start=True, stop=False), [pk, vk], [ok])
            P.op("pe", lambda e: e.matmul(o[:, 0:128], lhsT=p[:, 128:256], rhs=v[:, b + 1, :], start=False, stop=True), [pk, vk], [ok])
            P.op("dve", lambda e: e.tensor_copy(out=og[:, b, :], in_=o[:, 0:128]), [ok], [ogk + "u"])

        for b in range(n_blk + 1):
            if b < n_blk:
                unit1(b)
            if b >= 1:
                unit2(b - 1)
        P.dma(out_d[n].rearrange("b p d -> p b d"), og[:, :, :], reads=[ogk + "u"], writes=[ogk])

    for n in range(n_heads):
        head(n)


def build_D(n_slots=16, n_heads=12, do_sb=True, do_dil=True):
    nc = bass.Bass("TRN2", target_bir_lowering=False)
    S = n_slots * 1024
    NKB = S // 128
    with contextlib.ExitStack() as st:
        C = Ctx(nc, st)
        P = C.P
        sb = C.sb
        QT_d = C.dram("sQT", [64, n_slots * 512], BF16)
        KT_d = C.dram("sKT", [64, S], BF16)
        V_d = C.dram("sV", [128, NKB, 64], BF16)
        M01_d = C.dram("M01", [128, 8, 512], BF16)
        Mneg_d = C.dram("Mneg", [128, 8, 512], BF16)
        tri_d = C.dram("negtri", [128, 128], BF16)
        id_d = C.dram("ident", [128, 128], BF16)
        idf_d = C.dram("ident_f", [128, 128], F32)
        dQ_d = C.dram("dlQ", [n_heads, 65, 16 * 128], BF16)
        dK_d = C.dram("dlK", [n_heads, 65, 17 * 128], BF16)
        dV_d = C.dram("dlV", [n_heads, 128, 17, 128], BF16)
        Bt_d = C.dram("dlB", [128, n_heads, 2, 128], F32)
        osb_o = C.dram("o_sb", [n_slots * 512, 64], F32, "ExternalOutput")
        odl_o = C.dram("o_dl", [n_heads, 16, 128, 128], F32, "ExternalOutput")
        QT = sb("QT_s", [128, n_slots * 512], BF16)
        KT = sb("KT_s", [128, S], BF16)
        V = sb("V_s", [128, NKB, 64], BF16)
        M01 = sb("M01_s", [128, 8, 512], BF16)
        Mneg = sb("Mneg_s", [128, 8, 512], BF16)
        negtri = sb("negtri_s", [128, 128], BF16)
        negones = sb("negones_s", [128, 128], BF16)
        ident = sb("ident_s", [128, 128], BF16)
        ident_f = sb("identf_s", [128, 128], F32)
        Bt = sb("Bt_s", [128, n_heads, 2, 128], F32)
        one_t = sb("one_t", [128, 1], F32)
        P.op("pool", lambda e: e.memset(one_t[:], 1.0), [], ["one_t"])
        P.op("pool", lambda e: e.memset(negones[:], -8.0), [], ["negones"])
        for t, d, k in ((M01, M01_d, "M01"), (Mneg, Mneg_d, "Mneg"), (Bt, Bt_d, "Bt")):
            P.dma(t[:], d[:, :, :] if len(d.shape) == 3 else d[:, :, :, :], writes=[k])
        for t, d, k in ((negtri, tri_d, "negtri"), (ident, id_d, "ident"), (ident_f, idf_d, "ident_f")):
            P.dma(t[:], d[:, :], writes=[k])
        psA = Ring(C.psb, "psA", 2, [128, 512], F32)
        psB = Ring(C.psb, "psB", 2, [128, 512], F32)
        psO = Ring(C.psb, "psO", 2, [128, 512], F32)
        if do_sb:
            load_big(C, QT, QT_d, 64, "QT", n_slots * 512)
            load_big(C, KT, KT_d, 64, "KT", S)
            P.dma(V[:, :, :], V_d[:, :, :], writes=["V"], eng="pool")
            sb_stream(C, QT, KT, V, M01, Mneg, negtri, negones, ident, one_t, osb_o, psA, psB, psO, n_slots)
        if do_dil:
            dil_part(C, dQ_d, dK_d, dV_d, Bt, ident_f, odl_o, psA, psO, n_heads)
        P.emit(st)
    return nc


LAM_INIT0 = 0.8 - 0.6 * 1.0


def build_post(layer, T=2048):
    nc = bass.Bass("TRN2", target_bir_lowering=False)
    W = 512
    NT = T // W
    nA = 12 if layer == 0 else 14
    nK = 8 if layer == 0 else 4
    with contextlib.ExitStack() as st:
        C = Ctx(nc, st)
        P = C.P
        sb = C.sb
        xT = C.dram("xT", [1024, T], F32)
        aT = C.dram("aT", [nA * 128, T], F32)
        w_out = C.dram("w_out", [nK * 128, 1024], F32)
        gains_d = C.dram("gains", [128, 9], F32)
        lam_d = C.dram("lamv", [128, 4, 64], F32)
        xm_o = C.dram("xmT", [1024, T], F32, "ExternalOutput")
        w_b = sb("w_b", [128, nK, 1024], BF16)
        gains = sb("gains_t", [128, 9], F32)
        C.eps_t = sb("eps_t", [128, 1], F32)
        P.op("pool", lambda e: e.memset(C.eps_t[:], EPS), [], ["eps_t"])
        ones = C.ones_f32()
        P.dma(w_b[:], w_out.rearrange("(kc p) n -> p kc n", p=128), writes=["w_b"], eng="pool")
        P.dma(gains[:], gains_d[:, :], writes=["gains"])
        neg_lam = sb("neg_lam", [128, 1], F32)
        if layer == 0:
            lamv = sb("lamv_t", [128, 4, 64], F32)
            pr = sb("lam_pr", [128, 2, 64], F32)
            ls = sb("lam_s", [128, 4], F32)
            P.dma(lamv[:], lam_d[:, :, :], writes=["lamv"])
            for i in range(2):
                P.op("dve", lambda e, i=i: e.tensor_tensor(out=pr[:, i, :], in0=lamv[:, 2 * i, :], in1=lamv[:, 2 * i + 1, :], op=ALU.mult), ["lamv"], ["lam_pr"])
                P.op("dve", lambda e, i=i: e.tensor_reduce(out=ls[:, i:i + 1], in_=pr[:, i, :], axis=AX.X, op=ALU.add), ["lam_pr"], ["lam_s"])
                P.op("act", lambda e, i=i: e.activation(out=ls[:, 2 + i:3 + i], in_=ls[:, i:i + 1], func=AF.Exp), ["lam_s"], ["lam_e"])
            P.op("dve", lambda e: e.tensor_tensor(out=neg_lam[:, :], in0=ls[:, 3:4], in1=ls[:, 2:3], op=ALU.subtract), ["lam_e"], ["neg_lam"])
            P.op("dve", lambda e: e.tensor_scalar(out=neg_lam[:, :], in0=neg_lam[:, :], scalar1=-LAM_INIT0, scalar2=None, op0=ALU.add), ["neg_lam"], ["neg_lam"])
        xr = Ring(sb, "xt", 2, [128, 8, W], F32)
        ar = Ring(sb, "at", 2, [128, nA, W], F32)
        catr = Ring(sb, "cat", 2, [128, nK, W], BF16)
        hr = Ring(sb, "ht", 2, [128, 8, W], F32)
        psr = C.psum_ring(6)
        pss = Ring(C.psb, "pss", 2, [128, 512], F32)
        sqr = Ring(sb, "sq", 3, [128, W], F32)
        rsr = Ring(sb, "rs", 2, [128, W], F32)
        tmpr = Ring(sb, "tmp", 4, [128, W], F32)
        xT_v = xT.rearrange("(kc p) t -> p kc t", p=128)
        aT_v = aT.rearrange("(kc p) t -> p kc t", p=128)
        xm_v = xm_o.rearrange("(kc p) t -> p kc t", p=128)

        def tile_body(tt):
            c0 = tt * W
            xt, xk = xr.next()
            P.dma(xt[:], xT_v[:, :, c0:c0 + W], writes=[xk])
            at, ak = ar.next()
            half = nA // 2
            P.dma(at[:, :half, :], aT_v[:, :half, c0:c0 + W], writes=[ak + "a"], eng="pool")
            P.dma(at[:, half:, :], aT_v[:, half:, c0:c0 + W], writes=[ak + "b"])
            aks = [ak + "a", ak + "b"]
            cat, ck = catr.next()
            if layer == 0:
                for i in range(4):
                    P.op("pool", lambda e, i=i: e.tensor_copy(out=cat[:, i, :], in_=at[:, i, :]), aks, [ck + str(i)])
                for h in range(4):
                    dt_, dk_ = tmpr.next()
                    P.op("dve", lambda e, h=h, dt_=dt_: e.scalar_tensor_tensor(out=dt_[:, :], in0=at[:, 8 + h, :], scalar=neg_lam[:, 0:1], in1=at[:, 4 + h, :], op0=ALU.mult, op1=ALU.add), aks + ["neg_lam"], [dk_])
                    rms_T(C, "sub", [dt_[:, :]], [dk_], [gains[:, 0:1]], 128, [cat[:, 4 + h, :]], [ck + str(4 + h)], W, ones, pss, sqr, rsr, mul=1.0 - LAM_INIT0)
            else:
                for c2 in range(2):
                    ns, nk_ = tmpr.next()
                    ds, dk_ = tmpr.next()
                    P.op("pool", lambda e, c2=c2, ns=ns: e.tensor_tensor(out=ns[:, :], in0=at[:, c2, :], in1=at[:, 2 + c2, :], op=ALU.add), aks, [nk_])
                    P.op("pool", lambda e, c2=c2, ns=ns: e.tensor_tensor(out=ns[:, :], in0=ns[:, :], in1=at[:, 4 + c2, :], op=ALU.add), aks + [nk_], [nk_])
                    P.op("pool", lambda e, c2=c2, ds=ds: e.tensor_tensor(out=ds[:, :], in0=at[:, 6 + c2, :], in1=at[:, 8 + c2, :], op=ALU.add), aks, [dk_])
                    P.op("pool", lambda e, c2=c2, ds=ds: e.tensor_tensor(out=ds[:, :], in0=ds[:, :], in1=at[:, 10 + c2, :], op=ALU.add), aks + [dk_], [dk_])
                    P.op("dve", lambda e, ds=ds: e.reciprocal(out=ds[:, :], in_=ds[:, :]), [dk_], [dk_])
                    P.op("dve", lambda e, c2=c2, ns=ns, ds=ds: e.tensor_tensor(out=cat[:, c2, :], in0=ns[:, :], in1=ds[:, :], op=ALU.mult), [nk_, dk_], [ck + str(c2)])
                for i in range(2):
                    P.op("pool", lambda e, i=i: e.tensor_copy(out=cat[:, 2 + i, :], in_=at[:, 12 + i, :]), aks, [ck + str(2 + i)])
            cks = [ck + str(i) for i in range(nK)]
            ht, hk = hr.next()
            for oc in range(8):
                ps, pk = psr.next()
                for kc in range(nK):
                    P.op("pe", lambda e, kc=kc, ps=ps, oc=oc: e.matmul(ps[:, :], lhsT=w_b[:, kc, oc * 128:(oc + 1) * 128], rhs=cat[:, kc, :], start=(kc == 0), stop=(kc == nK - 1)), cks + ["w_b"], [pk])
                C.evac(ht[:, oc, :], ps[:, :], [pk], [hk])
            rms_T(C, "post", [ht[:, kc, :] for kc in range(8)], [hk] * 8, [gains[:, 1 + kc:2 + kc] for kc in range(8)], 1024,
                  [ht[:, kc, :] for kc in range(8)], [hk + "n"] * 8, W, ones, pss, sqr, rsr)
            P.op("pool", lambda e: e.tensor_tensor(out=xt[:, :, :], in0=xt[:, :, :], in1=ht[:, :, :], op=ALU.add), [xk, hk + "n", hk], [xk, hk])
            P.dma(xm_v[:, :, c0:c0 + W], xt[:, :, :], reads=[xk])

        for tt in range(NT):
            tile_body(tt)
        P.emit(st)
    return nc


def build_ffn(T=2048):
    nc = bass.Bass("TRN2", target_bir_lowering=False)
    W = 256
    NT = T // W
    W2 = W + 2
    with contextlib.ExitStack() as st:
        C = Ctx(nc, st)
        P = C.P
        sb = C.sb
        xT = C.dram("xmT", [1024, T + 2], F32)
        w_up = C.dram("w_up", [1024, 5632], F32)
        w_dn = C.dram("w_dn", [2816, 1024], F32)
        cw_d = C.dram("cw", [128, 44, 3], F32)
        cb_d = C.dram("cb", [128, 44], F32)
        gains_d = C.dram("gains", [128, 16], F32)
        xo = C.dram("xoT", [1024, T], F32, "ExternalOutput")
        wu_b = sb("wu_b", [128, 8, 5632], BF16)
        wd_b = sb("wd_b", [128, 22, 1024], BF16)
        cw = sb("cw_t", [128, 44, 3], F32)
        cb = sb("cb_t", [128, 44], F32)
        gains = sb("gains_t", [128, 16], F32)
        C.eps_t = sb("eps_t", [128, 1], F32)
        P.op("pool", lambda e: e.memset(C.eps_t[:], EPS), [], ["eps_t"])
        ones = C.ones_f32()
        wu_v = w_up.rearrange("(kc p) n -> p kc n", p=128)
        wd_v = w_dn.rearrange("(kc p) n -> p kc n", p=128)
        for g in range(11):
            for hlf in range(2):
                c_ = hlf * 2816 + 256 * g
                P.dma(wu_b[:, :, c_:c_ + 256], wu_v[:, :, c_:c_ + 256], writes=[f"wu{hlf}_{g}"], eng="pool")
        for i in range(0, 22, 2):
            P.dma(wd_b[:, i:i + 2, :], wd_v[:, i:i + 2, :], writes=[f"wd{i}"], eng="pool")
        P.dma(cw[:], cw_d[:, :, :], writes=["cw"])
        P.dma(cb[:], cb_d[:, :], writes=["cb"])
        P.dma(gains[:], gains_d[:, :], writes=["gains"])
        xr = Ring(sb, "xt", 2, [128, 8, W2], F32)
        xnr = Ring(sb, "xn", 2, [128, 8, W2], BF16)
        gtr = Ring(sb, "gt", 1, [128, 22, W], BF16)
        hr = Ring(sb, "ht", 1, [128, 8, W], F32)
        psr = C.psum_ring(6)
        pss = Ring(C.psb, "pss", 2, [128, 512], F32)
        sqr = Ring(sb, "sq", 3, [128, W2], F32)
        rsr = Ring(sb, "rs", 2, [128, W2], F32)
        accr = Ring(sb, "acc", 6, [128, W], F32)
        sgr = Ring(sb, "sg", 3, [128, W], F32)
        xT_v = xT.rearrange("(kc p) t -> p kc t", p=128)
        xo_v = xo.rearrange("(kc p) t -> p kc t", p=128)

        def tile_body(tt):
            s0 = tt * W
            xt, xk = xr.next()
            P.dma(xt[:], xT_v[:, :, s0:s0 + W2], writes=[xk])
            xn, xnk = xnr.next()
            rms_T(C, "pre", [xt[:, kc, :] for kc in range(8)], [xk] * 8, [gains[:, kc:kc + 1] for kc in range(8)], 1024,
                  [xn[:, kc, :] for kc in range(8)], [xnk] * 8, W2, ones, pss, sqr, rsr)
            gt, gk = gtr.next()

            def ff_chunk(i):
                accs = []
                for col0, ch in ((128 * i, i), (2816 + 128 * i, 22 + i)):
                    ps, pk = psr.next()
                    for kc in range(8):
                        P.op("pe", lambda e, kc=kc, ps=ps, col0=col0: e.matmul(ps[:, :W2], lhsT=wu_b[:, kc, col0:col0 + 128], rhs=xn[:, kc, :], start=(kc == 0), stop=(kc == 7)), [xnk, f"wu{0 if col0 < 2816 else 1}_{i // 2}"], [pk])
                    acc, acck = accr.next()
                    P.op("dve", lambda e, ps=ps, acc=acc, ch=ch: e.tensor_scalar(out=acc[:, :], in0=ps[:, 2:W2], scalar1=cw[:, ch, 2:3], scalar2=cb[:, ch:ch + 1], op0=ALU.mult, op1=ALU.add), [pk, "cw", "cb"], [acck])
                    P.op("dve", lambda e, ps=ps, acc=acc, ch=ch: e.scalar_tensor_tensor(out=acc[:, :], in0=ps[:, 1:W2 - 1], scalar=cw[:, ch, 1:2], in1=acc[:, :], op0=ALU.mult, op1=ALU.add), [pk, "cw", acck], [acck])
                    P.op("dve", lambda e, ps=ps, acc=acc, ch=ch: e.scalar_tensor_tensor(out=acc[:, :], in0=ps[:, 0:W], scalar=cw[:, ch, 0:1], in1=acc[:, :], op0=ALU.mult, op1=ALU.add), [pk, "cw", acck], [acck])
                    accs.append((acc, acck))
                sg, sgk = sgr.next()
                P.op("act", lambda e: e.activation(out=sg[:, :], in_=accs[0][0][:, :], func=AF.Silu), [accs[0][1]], [sgk])
                P.op("pool", lambda e: e.tensor_tensor(out=gt[:, i, :], in0=sg[:, :], in1=accs[1][0][:, :], op=ALU.mult), [sgk, accs[1][1]], [gk + str(i)])

            for i in range(22):
                ff_chunk(i)
            gks = [gk + str(i) for i in range(22)]
            ht, hk = hr.next()
            for oc in range(8):
                ps, pk = psr.next()
                for i in range(22):
                    P.op("pe", lambda e, i=i, ps=ps, oc=oc: e.matmul(ps[:, :W], lhsT=wd_b[:, i, oc * 128:(oc + 1) * 128], rhs=gt[:, i, :], start=(i == 0), stop=(i == 21)), gks + [f"wd{i - i % 2}"], [pk])
                C.evac(ht[:, oc, :], ps[:, :W], [pk], [hk])
            rms_T(C, "post", [ht[:, kc, :] for kc in range(8)], [hk] * 8, [gains[:, 8 + kc:9 + kc] for kc in range(8)], 1024,
                  [ht[:, kc, :] for kc in range(8)], [hk + "n"] * 8, W, ones, pss, sqr, rsr)
            P.op("pool", lambda e: e.tensor_tensor(out=ht[:, :, :], in0=xt[:, :, 2:W2], in1=ht[:, :, :], op=ALU.add), [xk, hk + "n", hk], [hk + "o"])
            P.dma(xo_v[:, :, s0:s0 + W], ht[:, :, :], reads=[hk + "o"], writes=[hk, hk + "n"] + gks)

        for tt in range(NT):
            tile_body(tt)
        P.emit(st)
    return nc


def build_pre1(T=2048):
    nc = bass.Bass("TRN2", target_bir_lowering=False)
    W = 512
    NT = T // W
    with contextlib.ExitStack() as st:
        C = Ctx(nc, st)
        P = C.P
        sb = C.sb
        xT = C.dram("xT", [1024, T], F32)
        w_in = C.dram("w_in", [1024, 3072], F32)
        gains_d = C.dram("gains", [128, 8], F32)
        fm_o = C.dram("fmT", [2048, T], BF16, "ExternalOutput")
        cv_o = C.dram("cv", [T, 768], BF16, "ExternalOutput")
        sv_o = C.dram("sv", [T, 256], BF16, "ExternalOutput")
        w_b = sb("w_b", [128, 8, 3072], BF16)
        gains = sb("gains_t", [128, 8], F32)
        C.eps_t = sb("eps_t", [128, 1], F32)
        P.op("pool", lambda e: e.memset(C.eps_t[:], EPS), [], ["eps_t"])
        ones = C.ones_f32()
        w_v = w_in.rearrange("(kc p) n -> p kc n", p=128)
        wkeys = []
        for kc in range(8):
            P.dma(w_b[:, kc, :], w_v[:, kc, :], writes=[f"w{kc}"], eng="pool")
            wkeys.append(f"w{kc}")
        P.dma(gains[:], gains_d[:, :], writes=["gains"])
        xr = Ring(sb, "xt", 2, [128, 8, W], F32)
        hnr = Ring(sb, "hn", 2, [128, 8, W], BF16)
        psr = C.psum_ring(6)
        pss = Ring(C.psb, "pss", 2, [128, 512], F32)
        sqr = Ring(sb, "sq", 3, [128, W], F32)
        rsr = Ring(sb, "rs", 2, [128, W], F32)
        stg = Ring(sb, "stg", 6, [128, W], BF16)
        xT_v = xT.rearrange("(kc p) t -> p kc t", p=128)
        fm_cols = [128 * i for i in range(12)] + [2304, 2432, 2560, 2688]

        def tile_body(tt):
            c0 = tt * W
            xt, xk = xr.next()
            P.dma(xt[:], xT_v[:, :, c0:c0 + W], writes=[xk])
            hn, hk = hnr.next()
            rms_T(C, "pre", [xt[:, kc, :] for kc in range(8)], [xk] * 8, [gains[:, kc:kc + 1] for kc in range(8)], 1024,
                  [hn[:, kc, :] for kc in range(8)], [hk] * 8, W, ones, pss, sqr, rsr)
            for oi, col in enumerate(fm_cols):
                ps, pk = psr.next()
                for kc in range(8):
                    P.op("pe", lambda e, kc=kc, ps=ps, col=col: e.matmul(ps[:, :], lhsT=w_b[:, kc, col:col + 128], rhs=hn[:, kc, :], start=(kc == 0), stop=(kc == 7)), [hk] + wkeys, [pk])
                so, sk = stg.next()
                C.evac(so[:, :], ps[:, :], [pk], [sk])
                P.dma(fm_o[128 * oi:128 * oi + 128, c0:c0 + W], so[:, :], reads=[sk])
            for tb in range(W // 128):
                for col, n, dst, dc in ((1536, 512, cv_o, 0), (2048, 256, cv_o, 512), (2816, 256, sv_o, 0)):
                    ps, pk = psr.next()
                    for kc in range(8):
                        P.op("pe", lambda e, kc=kc, ps=ps, col=col, n=n, tb=tb: e.matmul(ps[:, :n], lhsT=hn[:, kc, tb * 128:(tb + 1) * 128], rhs=w_b[:, kc, col:col + n], start=(kc == 0), stop=(kc == 7)), [hk] + wkeys, [pk])
                    so, sk = stg.next()
                    C.evac(so[:, :n], ps[:, :n], [pk], [sk])
                    P.dma(dst[c0 + tb * 128:c0 + tb * 128 + 128, dc:dc + n], so[:, :n], reads=[sk])

        for tt in range(NT):
            tile_body(tt)
        P.emit(st)
    return nc


def rope_tables(pos):
    half = 16
    inv = (np.float32(10000.0) ** (-(np.arange(half, dtype=np.float32)) / np.float32(half))).astype(np.float32)
    ang = pos.astype(np.float32)[:, None] * inv[None, :]
    cos, sin = np.cos(ang).astype(np.float32), np.sin(ang).astype(np.float32)
    T = pos.shape[0]
    t = np.zeros((128, 2, T), np.float32)
    t[64:80, 0] = cos.T; t[80:96, 0] = cos.T
    t[64:80, 1] = -sin.T; t[80:96, 1] = sin.T
    return t

def prep_A(inp):
    x = inp["x"][0]
    xT = np.ascontiguousarray(x.T)
    w_in = np.ascontiguousarray(inp["ev_w_in"][0])
    z = np.zeros((1024, 64), np.float32)
    kr = w_in[:, 384:416]
    w_kr = np.stack([np.concatenate([z, kr], 1), np.concatenate([z, kr[:, 16:], kr[:, :16]], 1)], 1)
    uq = inp["ev_w_uq"][0].reshape(256, 8, 96)
    uqs = np.concatenate([uq[:, :, :64], uq[:, :, 80:96], uq[:, :, 64:80]], 2)
    w_uq = np.ascontiguousarray(np.stack([uq, uqs], 1))
    ukv = inp["ev_w_ukv"][0].reshape(128, 8, 128)
    w_uk = np.ascontiguousarray(ukv[:, :, :64].reshape(128, 512))
    w_uv = np.ascontiguousarray(ukv[:, :, 64:].reshape(128, 512))
    gains = np.concatenate([inp["ev_pre_g"][0].reshape(8, 128).T, inp["ev_cq_g"][0].reshape(2, 128).T, inp["ev_ckv_g"][0].reshape(1, 128).T], 1)
    gains = np.ascontiguousarray(gains.astype(np.float32))
    maps = []
    for c in range(8):
        pos = np.arange(2048 * c, 2048 * c + 2048)
        maps.append(dict(xT=np.ascontiguousarray(xT[:, 2048 * c:2048 * c + 2048]), w_in=w_in, w_kr=np.ascontiguousarray(w_kr), w_uq=w_uq,
                         w_uk=w_uk, w_uv=w_uv, gains=gains, rope_t=rope_tables(pos)))
    return maps

def bf(a):
    return np.ascontiguousarray(a).astype(NPBF)

def consts_B():
    k = np.arange(128)[:, None]; q = np.arange(128)[None, :]
    maskb = np.where(k <= q, 0.0, -30000.0).astype(NPBF)
    ident = np.eye(128, dtype=np.float32).astype(NPBF)
    return maskb, ident

def v_layout(v, S):
    d = v.shape[1]
    o = np.ones((128, S // 128, d + 1), NPBF)
    o[:, :, :d] = v.reshape(S // 128, 128, d).transpose(1, 0, 2)
    return o

def prep_B(A, S):
    maskb, ident = consts_B()
    t = np.arange(S)
    maps = []
    for c in range(8):
        h, j = c // 2, c % 2
        slope = 2.0 ** (-2.0 * (h + 1))
        qa = np.stack([-slope * 128 * (t // 128) * 8, -slope * (t % 128) * 8, np.ones(S), np.ones(S)]).astype(NPBF)
        ka = np.stack([np.ones(S), np.ones(S), slope * 128 * (t // 128) * 8, slope * (t % 128) * 8]).astype(NPBF)
        r0 = h * 128 + j * 64
        maps.append(dict(
            QT=bf(A["qT"][c]), KT=bf(np.concatenate([A["knT"][c * 64:(c + 1) * 64], A["krT"]], 0)),
            V1=v_layout(A["vm"][:, c * 64:(c + 1) * 64], S),
            dQT=bf(np.concatenate([A["dqT"][r0:r0 + 64], qa], 0)), dKT=bf(np.concatenate([A["dkT"][r0:r0 + 64], ka], 0)),
            dV1=v_layout(A["dv"][:, h * 128:(h + 1) * 128], S), maskb=maskb, ident=ident))
    return maps

DIL = ((128, 1), (512, 4), (2048, 16))

def consts_D(half):
    k = np.arange(128)[:, None, None]; j = np.arange(8)[None, :, None]; q = np.arange(512)[None, None, :]
    valid = (128 * j + k) < (512 * half + q)
    M01 = valid.astype(np.float32).astype(NPBF)
    Mneg = np.where(valid, 0.0, -30000.0).astype(NPBF)
    jj = np.arange(128)[:, None]; kk = np.arange(128)[None, :]
    negtri = np.where(jj >= kk, -8.0, 0.0).astype(NPBF)
    return M01, Mneg, negtri

def dil_bias():
    k = np.arange(128)[:, None].astype(np.float64); q = np.arange(128)[None, :].astype(np.float64)
    Bt = np.zeros((128, 12, 2, 128), np.float32)
    for n in range(12):
        g = n // 4
        d = DIL[g][1]
        slope = 2.0 ** (-8.0 * (n + 1) / 12)
        Bt[:, n, 0, :] = np.where(k >= q, -slope * d * (q + 128 - k) * 8, -240000.0)
        Bt[:, n, 1, :] = np.where(k <= q, -slope * d * (q - k) * 8, -240000.0)
    return Bt

def dil_perm(S, d):
    L = S // d
    pi = np.arange(S)
    return d * (pi % L) + (pi // L)

def prep_D(L1, S, n_slots=16):
    _, ident = consts_B()
    ident_f = np.eye(128, dtype=np.float32)
    Bt = dil_bias()
    S_sb = n_slots * 1024
    maps = []
    perms = [dil_perm(S, d) for _, d in DIL]
    for c in range(8):
        h, half = c // 2, c % 2
        M01, Mneg, negtri = consts_D(half)
        qcols = np.concatenate([np.arange(512 * (2 * i + half), 512 * (2 * i + half) + 512) for i in range(n_slots)])
        sQT = bf(L1["sqT"][h * 64:(h + 1) * 64][:, qcols])
        sKT = bf(L1["skT"][h * 64:(h + 1) * 64][:, :S_sb])
        sV = bf(L1["sv"][:S_sb, h * 64:(h + 1) * 64].reshape(S_sb // 128, 128, 64).transpose(1, 0, 2))
        dlQ = np.zeros((12, 65, 2048), NPBF); dlK = np.zeros((12, 65, 2176), NPBF); dlV = np.zeros((12, 128, 17, 128), NPBF)
        for n in range(12):
            g = n // 4
            d = DIL[g][1]; Lc = S // d
            tp = perms[g]
            pi = np.arange(2048 * c, 2048 * c + 2048)
            dlQ[n, :64] = L1["cqT"][n * 64:(n + 1) * 64][:, tp[pi]]
            dlQ[n, 64] = np.where((pi % Lc) < 128, -30000.0, 0.0).astype(NPBF)
            pk = np.arange(2048 * c - 128, 2048 * c + 2048)
            ok = pk >= 0
            tk = tp[np.maximum(pk, 0)]
            kk = np.asarray(L1["ckT"][n * 64:(n + 1) * 64][:, tk]).copy(); kk[:, ~ok] = 0
            dlK[n, :64] = kk; dlK[n, 64] = 1.0
            vv = np.asarray(L1["cv"][tk, n * 64:(n + 1) * 64]).copy(); vv[~ok] = 0
            dlV[n, :, :, :64] = vv.reshape(17, 128, 64).transpose(1, 0, 2); dlV[n, :, :, 64:] = 1.0
        maps.append(dict(sQT=sQT, sKT=sKT, sV=sV, M01=M01, Mneg=Mneg, negtri=negtri, ident=ident, ident_f=ident_f,
                         dlQ=dlQ, dlK=dlK, dlV=dlV, dlB=Bt))
    return maps

def post_D(res, S, n_slots=16):
    S_sb = n_slots * 1024
    o_sb = np.zeros((S_sb, 256), np.float32)
    for c in range(8):
        h, half = c // 2, c % 2
        o = np.asarray(res[c]["o_sb"]).reshape(n_slots, 512, 64)
        for i in range(n_slots):
            o_sb[512 * (2 * i + half):512 * (2 * i + half) + 512, h * 64:(h + 1) * 64] = o[i]
    odl = np.concatenate([np.asarray(res[c]["o_dl"]).reshape(12, 2048, 128) for c in range(8)], 1)
    num = [np.zeros((S, 256), np.float32) for _ in range(3)]; den = [np.zeros((S, 256), np.float32) for _ in range(3)]
    for n in range(12):
        g, j = n // 4, n % 4
        tp = dil_perm(S, DIL[g][1])
        num[g][tp, j * 64:(j + 1) * 64] = odl[n, :, :64]
        den[g][tp, j * 64:(j + 1) * 64] = odl[n, :, 64:]
    return o_sb, num, den


def _run(nc, maps, tag=""):
    res = run_bass_kernel_spmd(nc, maps, core_ids=list(range(8)))
    et = getattr(res, "exec_time_ns", None)
    if et is not None:
        import sys
        print(f"[kernel] launch {tag}: exec_time_ns={et}", file=sys.stderr, flush=True)
    return res.results


def _chunks(g, n):
    return np.ascontiguousarray(np.asarray(g, np.float32).reshape(n, 128).T)


def kernel(**inp):
    inp = {k: np.asarray(v) for k, v in inp.items()}
    S = 16384
    T = 2048
    f32 = np.float32
    RA = _run(build_A(), prep_A(inp), "A")
    A = {}
    for k, ax in (("qT", 2), ("krT", 1), ("knT", 1), ("vm", 0), ("dqT", 1), ("dkT", 1), ("dv", 0)):
        A[k] = np.concatenate([np.asarray(RA[c][k]) for c in range(8)], ax)
    RB = _run(build_B(), prep_B(A, S), "B")
    aT = np.concatenate([np.asarray(RB[h]["o_mla"]).T for h in range(8)]
                        + [np.asarray(RB[2 * h]["o_d"]).T for h in range(4)]
                        + [np.asarray(RB[2 * h + 1]["o_d"]).T for h in range(4)], 0).astype(f32)
    xT = np.ascontiguousarray(inp["x"][0].T)
    out = None
    for layer in range(2):
        if layer == 0:
            w_out = np.ascontiguousarray(inp["ev_w_out"][0])
            gains = np.concatenate([_chunks(inp["ev_subln_g"][0], 1), _chunks(inp["ev_post_g"][0], 8)], 1)
            lamv = np.stack([inp["ev_lam_q1"][0], inp["ev_lam_k1"][0], inp["ev_lam_q2"][0], inp["ev_lam_k2"][0]], 0).astype(f32)
        else:
            w_out = np.ascontiguousarray(inp["od_w_out"][0])
            gains = np.concatenate([np.ones((128, 1), f32), _chunks(inp["od_post_g"][0], 8)], 1)
            lamv = np.zeros((4, 64), f32)
        lamv = np.ascontiguousarray(np.broadcast_to(lamv[None], (128, 4, 64)))
        gains = np.ascontiguousarray(gains.astype(f32))
        maps = [dict(xT=np.ascontiguousarray(xT[:, T * c:T * c + T]), aT=np.ascontiguousarray(aT[:, T * c:T * c + T]),
                     w_out=w_out, gains=gains, lamv=lamv) for c in range(8)]
        RC = _run(build_post(layer), maps, f"post{layer}")
        xmT = np.concatenate([np.asarray(RC[c]["xmT"]) for c in range(8)], 1)
        xh = np.concatenate([np.zeros((1024, 2), f32), xmT], 1)
        cw = np.ascontiguousarray(inp["ffn_conv_w"][layer].reshape(3, 44, 128).transpose(2, 1, 0)).astype(f32)
        cb = _chunks(inp["ffn_conv_b"][layer], 44)
        gains = np.ascontiguousarray(np.concatenate([_chunks(inp["ffn_pre_g"][layer], 8), _chunks(inp["ffn_post_g"][layer], 8)], 1))
        w_up = np.ascontiguousarray(inp["ffn_w_up"][layer]); w_dn = np.ascontiguousarray(inp["ffn_w_down"][layer])
        maps = [dict(xmT=np.ascontiguousarray(xh[:, T * c:T * c + T + 2]), w_up=w_up, w_dn=w_dn, cw=cw, cb=cb, gains=gains) for c in range(8)]
        RF = _run(build_ffn(), maps, f"ffn{layer}")
        xT = np.concatenate([np.asarray(RF[c]["xoT"]) for c in range(8)], 1)
        if layer == 0:
            maps = [dict(xT=np.ascontiguousarray(xT[:, T * c:T * c + T]), w_in=np.ascontiguousarray(inp["od_w_in"][0]),
                         gains=_chunks(inp["od_pre_g"][0], 8)) for c in range(8)]
            RP = _run(build_pre1(), maps, "pre1")
            fm = np.concatenate([np.asarray(RP[c]["fmT"]) for c in range(8)], 1)
            L1 = dict(cqT=fm[0:768], ckT=fm[768:1536], sqT=fm[1536:1792], skT=fm[1792:2048],
                      cv=np.concatenate([np.asarray(RP[c]["cv"]) for c in range(8)], 0),
                      sv=np.concatenate([np.asarray(RP[c]["sv"]) for c in range(8)], 0))
            RD = _run(build_D(), prep_D(L1, S), "D")
            o_sb, num, den = post_D(RD, S)
            aT = np.concatenate([n.T for n in num] + [d.T for d in den] + [o_sb.T], 0).astype(f32)
    return np.ascontiguousarray(xT.T)[None].astype(f32)
```
